# Optimizing a Trainium2 kernel written in Bass

```python
import jax, jax.numpy as jnp
from jax import lax
import numpy as np

D_MODEL = 1024
BATCH = 8
SEQ = 8192
DEPTH = 1

N_META = 16
RET_HEADS = 8
RET_QK_DIM = 128
RET_V_DIM = 128
RET_CHUNK = 128
ROPE_BASE = 10000.0
LRU_WIDTH = D_MODEL
LRU_BLOCKS = 4
LRU_BLOCK = LRU_WIDTH // LRU_BLOCKS
CONV_WIDTH = 4
LRU_C = 8.0
FFN_HIDDEN = ((8 * D_MODEL // 3 + 255) // 256) * 256
RET_QK = RET_HEADS * RET_QK_DIM
RET_V = RET_HEADS * RET_V_DIM
IN_COLS = 2 * RET_QK + 2 * RET_V + 2 * LRU_WIDTH + 2 * D_MODEL
NORM_EPS = 1e-6

kernel_name = 'hybrid_retention_rglru_gated_block'


def rmsnorm(x, w):
    xf = x.astype(jnp.float32)
    y = xf * lax.rsqrt(jnp.mean(xf * xf, axis=-1, keepdims=True) + NORM_EPS)
    return (y * w.astype(jnp.float32)).astype(x.dtype)


def rotary(x, pos):
    d = x.shape[-1]
    inv_freq = ROPE_BASE ** (-jnp.arange(0, d, 2, dtype=jnp.float32) / d)
    ang = pos.astype(jnp.float32)[:, None] * inv_freq[None, :]
    cos = jnp.cos(ang)[None, :, None, :]
    sin = jnp.sin(ang)[None, :, None, :]
    x1, x2 = x[..., : d // 2], x[..., d // 2:]
    return jnp.concatenate([x1 * cos - x2 * sin, x2 * cos + x1 * sin], axis=-1)


def chunk_retention(q, k, v):
    B, T, H, dk = q.shape
    dv = v.shape[-1]
    C = RET_CHUNK
    pad = C - N_META
    padw = ((0, 0), (pad, 0), (0, 0), (0, 0))
    q, k, v = jnp.pad(q, padw), jnp.pad(k, padw), jnp.pad(v, padw)
    Tp = T + pad
    n_chunks = Tp // C
    log_g = jnp.log(1.0 - 2.0 ** (-5.0 - jnp.arange(H, dtype=jnp.float32)))
    idx = jnp.arange(C, dtype=jnp.float32)
    diff = idx[:, None] - idx[None, :]
    intra = jnp.where(diff[None] >= 0,
                      jnp.exp(jnp.maximum(diff, 0.0)[None] * log_g[:, None, None]), 0.0)
    q_decay = jnp.exp((idx + 1.0)[:, None] * log_g[None, :])
    k_decay = jnp.exp((C - 1.0 - idx)[:, None] * log_g[None, :])
    chunk_decay = jnp.exp(C * log_g)

    def to_chunks(a):
        return a.reshape(B, n_chunks, C, H, a.shape[-1]).transpose(1, 0, 2, 3, 4)

    def step(state, qkv):
        qc, kc, vc = qkv
        s = jnp.einsum('bchd,bmhd->bhcm', qc, kc) * intra[None]
        inner = jnp.einsum('bhcm,bmhe->bche', s, vc)
        cross = jnp.einsum('bchd,bhde->bche', qc, state) * q_decay[None, :, :, None]
        state = state * chunk_decay[None, :, None, None] + jnp.einsum(
            'bmhd,bmhe->bhde', kc * k_decay[None, :, :, None], vc)
        return state, inner + cross

    s0 = jnp.zeros((B, H, dk, dv), jnp.float32)
    _, ys = lax.scan(step, s0, (to_chunks(q), to_chunks(k), to_chunks(v)))
    out = ys.transpose(1, 0, 2, 3, 4).reshape(B, Tp, H, dv)
    return out[:, pad:]


def causal_depthwise_conv(x, w, b):
    T = x.shape[1]
    xp = jnp.pad(x, ((0, 0), (CONV_WIDTH - 1, 0), (0, 0)))
    y = b[None, None, :]
    for j in range(CONV_WIDTH):
        y = y + xp[:, j:j + T] * w[j][None, None, :]
    return y


def rg_lru(x, wa, ba, wx, bx, lam):
    B, T, W = x.shape
    xb = x.reshape(B, T, LRU_BLOCKS, LRU_BLOCK)
    r = jax.nn.sigmoid(jnp.einsum('btgi,gij->btgj', xb, wa).reshape(B, T, W) + ba)
    i = jax.nn.sigmoid(jnp.einsum('btgi,gij->btgj', xb, wx).reshape(B, T, W) + bx)
    log_a = -LRU_C * r * jax.nn.softplus(-lam)
    a = jnp.exp(log_a)
    u = jnp.sqrt(-jnp.expm1(2.0 * log_a)) * (i * x)

    def step(h, au):
        a_t, u_t = au
        h = a_t * h + u_t
        return h, h

    _, hs = lax.scan(step, jnp.zeros((B, W), jnp.float32),
                     (a.transpose(1, 0, 2), u.transpose(1, 0, 2)))
    return hs.transpose(1, 0, 2)


def setup_inputs(seed: int = 0) -> dict:
    key = jax.random.key(seed)
    ks = jax.random.split(key, 20)
    f32 = jnp.float32
    nrm = lambda k, s, sc: jax.random.normal(k, s, f32) * sc
    a0 = jax.random.uniform(ks[9], (DEPTH, LRU_WIDTH), f32, minval=0.9, maxval=0.999)
    a0r = a0 ** (1.0 / LRU_C)
    return {
        'x': nrm(ks[0], (BATCH, SEQ, D_MODEL), 1.0),
        'meta_tokens': nrm(ks[1], (N_META, D_MODEL), 1.0),
        'mix_norm_w': 1.0 + nrm(ks[2], (DEPTH, D_MODEL), 0.02),
        'w_in': nrm(ks[3], (DEPTH, D_MODEL, IN_COLS), D_MODEL ** -0.5),
        'conv_w': nrm(ks[4], (DEPTH, CONV_WIDTH, LRU_WIDTH), CONV_WIDTH ** -0.5),
        'conv_b': nrm(ks[5], (DEPTH, LRU_WIDTH), 0.01),
        'lru_wa': nrm(ks[6], (DEPTH, LRU_BLOCKS, LRU_BLOCK, LRU_BLOCK), LRU_BLOCK ** -0.5),
        'lru_ba': nrm(ks[7], (DEPTH, LRU_WIDTH), 0.01),
        'lru_wx': nrm(ks[8], (DEPTH, LRU_BLOCKS, LRU_BLOCK, LRU_BLOCK), LRU_BLOCK ** -0.5),
        'lru_bx': nrm(ks[10], (DEPTH, LRU_WIDTH), 0.01),
        'lru_lambda': jnp.log(a0r) - jnp.log1p(-a0r),
        'w_branch_ret': nrm(ks[11], (DEPTH, RET_V, D_MODEL), RET_V ** -0.5),
        'w_branch_lru': nrm(ks[12], (DEPTH, LRU_WIDTH, D_MODEL), LRU_WIDTH ** -0.5),
        'w_out': nrm(ks[13], (DEPTH, D_MODEL, D_MODEL), D_MODEL ** -0.5),
        'ffn_norm_w': 1.0 + nrm(ks[14], (DEPTH, D_MODEL), 0.02),
        'w_ffn_in': nrm(ks[15], (DEPTH, D_MODEL, 2 * FFN_HIDDEN), D_MODEL ** -0.5),
        'w_ffn_out': nrm(ks[16], (DEPTH, FFN_HIDDEN, D_MODEL), FFN_HIDDEN ** -0.5),
        'final_norm_w': 1.0 + nrm(ks[17], (D_MODEL,), 0.02),
    }


def reference(x, meta_tokens, mix_norm_w, w_in, conv_w, conv_b, lru_wa, lru_ba, lru_wx,
              lru_bx, lru_lambda, w_branch_ret, w_branch_lru, w_out, ffn_norm_w,
              w_ffn_in, w_ffn_out, final_norm_w):
    B = x.shape[0]
    f32 = jnp.float32
    meta = jnp.broadcast_to(meta_tokens.astype(x.dtype)[None], (B, N_META, D_MODEL))
    h = jnp.concatenate([meta, x], axis=1)
    T = h.shape[1]
    pos = jnp.arange(T)
    sizes = (RET_QK, RET_QK, RET_V, RET_V, LRU_WIDTH, LRU_WIDTH, D_MODEL, D_MODEL)
    split_at = [int(s) for s in np.cumsum(sizes)[:-1]]
    for l in range(DEPTH):
        u = rmsnorm(h, mix_norm_w[l])
        proj = u @ w_in[l]
        q, k, v, g_ret, lru_in, lru_gate, gate_a, gate_b = jnp.split(proj, split_at, axis=-1)
        q = rotary(q.reshape(B, T, RET_HEADS, RET_QK_DIM).astype(f32), pos)
        k = rotary(k.reshape(B, T, RET_HEADS, RET_QK_DIM).astype(f32), pos) * (RET_QK_DIM ** -0.5)
        v = v.reshape(B, T, RET_HEADS, RET_V_DIM).astype(f32)
        o = chunk_retention(q, k, v)
        o = o * lax.rsqrt(jnp.mean(o * o, axis=-1, keepdims=True) + NORM_EPS)
        o = o.reshape(B, T, RET_V).astype(h.dtype)
        y_ret = (jax.nn.silu(g_ret) * o) @ w_branch_ret[l]
        c = causal_depthwise_conv(lru_in.astype(f32), conv_w[l].astype(f32), conv_b[l].astype(f32))
        r = rg_lru(c, lru_wa[l].astype(f32), lru_ba[l].astype(f32), lru_wx[l].astype(f32),
                   lru_bx[l].astype(f32), lru_lambda[l].astype(f32)).astype(h.dtype)
        y_lru = (jax.nn.gelu(lru_gate) * r) @ w_branch_lru[l]
        mixed = jax.nn.sigmoid(gate_a) * y_ret + jax.nn.sigmoid(gate_b) * y_lru
        h = h + mixed @ w_out[l]
        u = rmsnorm(h, ffn_norm_w[l])
        gu = u @ w_ffn_in[l]
        g, up = gu[..., :FFN_HIDDEN], gu[..., FFN_HIDDEN:]
        h = h + (jax.nn.silu(g) * up) @ w_ffn_out[l]
    h = rmsnorm(h, final_norm_w)
    return h[:, N_META:]
```

```python
import contextlib
import numpy as np
import concourse.bass as bass
import concourse.mybir as mybir
from concourse.bass_utils import run_bass_kernel_spmd

F32 = mybir.dt.float32
BF16 = mybir.dt.bfloat16
ALU = mybir.AluOpType
AF = mybir.ActivationFunctionType

D = 1024
NMETA = 16
NH = 8
FF = 2816
NFF = FF // 128
NT = 512
EPS = 1e-6
SEM_LIMIT = 30000
NWS = 8
NTMP = 12
NTB = 4
NPS = 8


class Counter:
    def __init__(self, sched, step):
        self.sched = sched
        self.step = step
        self.sems = [sched.new_sem()]
        self.val = 0

    def next_token(self):
        if self.val + self.step > SEM_LIMIT:
            self.sems.append(self.sched.new_sem())
            self.val = 0
        self.val += self.step
        return (self, len(self.sems) - 1, self.val)


class Sched:
    ENGS = ("pe", "act", "dve", "pool", "sp")

    def __init__(self, nc, stack, n_sems):
        self.nc = nc
        self.free_sems = [stack.enter_context(nc.semaphore(f"s{i}")) for i in range(n_sems)]
        self.streams = {e: [] for e in self.ENGS}
        self.ectr = {e: Counter(self, 1) for e in self.ENGS}
        self.known = {e: {} for e in self.ENGS}
        self.last_w = {}
        self.readers = {}
        self.dma_ctr = {}

    def new_sem(self):
        return self.free_sems.pop()

    def _deps(self, eng, reads, writes):
        deps = []
        for r in reads:
            t = self.last_w.get(r)
            if t is not None:
                deps.append(t)
            if isinstance(r, tuple) and r[0] == "ps":
                for t in self.readers.get(r, {}).values():
                    if t[0] is not self.ectr[eng]:
                        deps.append(t)
        for w in writes:
            t = self.last_w.get(w)
            if t is not None:
                deps.append(t)
            for t in self.readers.get(w, {}).values():
                if t[0] is self.ectr[eng]:
                    continue
                deps.append(t)
        best = {}
        for (c, ep, v) in deps:
            cur = best.get(c)
            if cur is None or (ep, v) > cur:
                best[c] = (ep, v)
        out = []
        kn = self.known[eng]
        for c, (ep, v) in best.items():
            cur = kn.get(c)
            if cur is not None and cur >= (ep, v):
                continue
            kn[c] = (ep, v)
            out.append((c.sems[ep], v))
        return out

    def _commit(self, tok, reads, writes):
        for r in reads:
            self.readers.setdefault(r, {})[tok[0]] = tok
        for w in writes:
            self.last_w[w] = tok
            self.readers[w] = {}

    def _skip(self, eng, fn):
        import os
        self.nops = getattr(self, "nops", 0) + 1
        mx = int(os.environ.get("KMAXOPS", "0"))
        if mx and self.nops == mx:
            print("LAST OP", self.nops, eng, "line", fn.__code__.co_firstlineno)
        return bool(mx) and self.nops > mx

    def op(self, eng, fn, reads=(), writes=()):
        if self._skip(eng, fn):
            return None
        waits = self._deps(eng, reads, writes)
        tok = self.ectr[eng].next_token()
        self.streams[eng].append((waits, fn, (tok[0].sems[tok[1]], 1)))
        self._commit(tok, reads, writes)
        return tok

    def dma(self, eng, fn, key, reads=(), writes=()):
        if self._skip(eng, fn):
            return None
        writes = list(writes) + [("dmakey", key)]
        waits = self._deps(eng, reads, writes)
        ctr = self.dma_ctr.get(key)
        if ctr is None:
            ctr = self.dma_ctr[key] = Counter(self, 16)
        tok = ctr.next_token()
        self.streams[eng].append((waits, fn, (tok[0].sems[tok[1]], 16)))
        self._commit(tok, reads, writes)
        return tok

    def final_wait(self, eng):
        waits = []
        for c in list(self.ectr.values()) + list(self.dma_ctr.values()):
            if c.val > 0:
                waits.append((c.sems[-1], c.val))
        self.streams[eng].append((waits, None, None))

    def emit(self, block):
        attr = {"pe": "tensor", "act": "scalar", "dve": "vector", "pool": "gpsimd", "sp": "sync"}
        for e in self.ENGS:
            stream = self.streams[e]

            def body(engine, stream=stream):
                for waits, fn, inc in stream:
                    for (sem, v) in waits:
                        engine.wait_ge(sem, v)
                    if fn is not None:
                        ins = fn(engine)
                        ins.then_inc(inc[0], inc[1])
            getattr(block, attr[e])(body)


def build(n_tiles):
    nc = bass.Bass("TRN2", target_bir_lowering=False)
    T = NMETA + n_tiles * NT

    def din(name, shape):
        return nc.dram_tensor(name, shape, F32, kind="ExternalInput").ap()

    x = din("x", [n_tiles * NT, D])
    meta = din("meta_tokens", [NMETA, D])
    w_in = din("w_in", [D, 8 * D])
    w_br = din("w_branch_ret", [D, D])
    w_bl = din("w_branch_lru", [D, D])
    w_o = din("w_out", [D, D])
    w_a = din("lru_wa", [4, 256, 256])
    w_x = din("lru_wx", [4, 256, 256])
    w_fi = din("w_ffn_in", [D, 2 * FF])
    w_fo = din("w_ffn_out", [FF, D])
    pv_d = din("pv", [128, 88])
    cs_d = din("cs_tab", [128, 2, T])
    mask_d = din("maskT", [128, 8, 128])
    qd_d = din("qd", [128, 8, 128])
    kd_d = din("kd", [128, 16])
    ident_d = din("ident", [128, 128])
    swap_d = din("swapP", [128, 128])
    out = nc.dram_tensor("out", [n_tiles * NT, D], F32, kind="ExternalOutput").ap()

    def dscr(name, shape):
        return nc.dram_tensor(name, shape, BF16, kind="Internal").ap()

    ws = {
        "in": dscr("ws_in", [64, 128, 8, 128]),
        "br": dscr("ws_br", [8, 128, 8, 128]),
        "bl": dscr("ws_bl", [8, 128, 8, 128]),
        "o": dscr("ws_o", [8, 128, 8, 128]),
        "a": dscr("ws_a", [8, 128, 2, 128]),
        "x": dscr("ws_x", [8, 128, 2, 128]),
        "fi": dscr("ws_fi", [44, 128, 8, 128]),
        "fo": dscr("ws_fo", [8, 128, NFF, 128]),
    }

    with contextlib.ExitStack() as st:
        S = Sched(nc, st, n_sems=96)

        def sb(name, shape, dt=F32):
            return st.enter_context(nc.sbuf_tensor(name, shape, dt))

        pv = sb("pv_s", [128, 96])
        ident = sb("ident_s", [128, 128])
        ident_bf = sb("ident_bf", [128, 128], BF16)
        ones_bf = sb("ones_bf", [128, 128], BF16)
        swap_bf = sb("swap_bf", [128, 128], BF16)
        maskT = sb("maskT_s", [128, 8, 128])
        qd = sb("qd_s", [128, 8, 128])
        kd = sb("kd_s", [128, 16])
        drv = sb("drv", [128, 48])
        state = sb("state", [128, 8, 128])
        state_bf = sb("state_bf", [128, 8, 128], BF16)
        hst = sb("hst", [128, 8])
        halo = sb("halo", [128, 8, 3])
        xin = [sb(f"xin{i}", [128, D]) for i in range(2)]
        otm = [sb(f"otm{i}", [128, D]) for i in range(2)]
        h = sb("h", [128, 8, NT])
        u = sb("u", [128, 8, NT], BF16)
        cs_t = sb("cs_t", [128, 2, NT])
        BB = sb("BB", [128, 24, NT], BF16)
        B2 = sb("B2", [128, 8, NT], BF16)
        k_dec = sb("k_dec", [128, 4, 8, 128], BF16)
        v_tm = sb("v_tm", [128, 4, D], BF16)
        yg = sb("yg", [128, 8, NT], BF16)
        ylg = sb("ylg", [128, 8, NT], BF16)
        lin = sb("lin", [128, 2, NT + 3])
        cc = sb("cc", [128, 2, NT])
        la = sb("la", [128, 2, NT])
        la2 = sb("la2", [128, 2, NT])
        luu = sb("luu", [128, 2, NT])
        stT = [sb(f"stT{i}", [128, 8, 128], BF16) for i in range(2)]
        tmp = [sb(f"tmp{i}", [128, NT]) for i in range(NTMP)]
        tmb = [sb(f"tmb{i}", [128, NT], BF16) for i in range(NTB)]
        wsl = [sb(f"wsl{i}", [128, 8, 128], BF16) for i in range(NWS)]
        ps = [st.enter_context(nc.psum_tensor(f"ps{i}", [128, 512], F32)) for i in range(NPS)]

        rr = {"ps": 0, "tmp": 0, "tmb": 0, "w": 0, "st": 0, "cv": 0}

        def bank():
            b = rr["ps"]
            rr["ps"] = (b + 1) % NPS
            return b

        def tslot():
            b = rr["tmp"]
            rr["tmp"] = (b + 1) % NTMP
            return b

        def bslot():
            b = rr["tmb"]
            rr["tmb"] = (b + 1) % NTB
            return b

        def ld(dst, src, key):
            S.dma("sp", lambda e: e.dma_start(out=dst, in_=src), key, writes=[key])

        ld(pv[:, 0:88], pv_d, "pv")
        ld(ident[:], ident_d, "ident")
        ld(maskT[:], mask_d, "maskT")
        ld(qd[:], qd_d, "qd")
        ld(kd[:], kd_d, "kd")
        S.dma("pool", lambda e: e.dma_start(out=ident_bf[:], in_=ident_d), "ident_bf", writes=["ident_bf"])
        S.dma("pool", lambda e: e.dma_start(out=swap_bf[:], in_=swap_d), "swap_bf", writes=["swap_bf"])
        S.op("pool", lambda e: e.memset(ones_bf[:], 1.0), writes=["ones_bf"])
        S.op("pool", lambda e: e.memset(halo[:], 0.0), writes=["halo"])
        S.op("pool", lambda e: e.memset(hst[:], 0.0), writes=["hst"])
        S.op("dve", lambda e: e.tensor_scalar(out=drv[:, 0:16], in0=pv[:, 64:80], scalar1=0.5, scalar2=None, op0=ALU.mult),
             reads=["pv"], writes=["drv"])
        S.op("act", lambda e: e.activation(out=drv[:, 32:40], in_=pv[:, 80:88], func=AF.Exp, scale=-1.0),
             reads=["pv"], writes=["drv_t"])
        S.op("act", lambda e: e.activation(out=drv[:, 40:48], in_=drv[:, 32:40], func=AF.Ln, bias=1.0),
             reads=["drv_t"], writes=["drv_t2"])
        S.op("dve", lambda e: e.tensor_scalar(out=drv[:, 16:24], in0=drv[:, 40:48], scalar1=-8.0, scalar2=None, op0=ALU.mult),
             reads=["drv_t2", "drv"], writes=["drv"])
        S.op("dve", lambda e: e.tensor_scalar(out=drv[:, 24:32], in0=drv[:, 40:48], scalar1=-4.0, scalar2=None, op0=ALU.mult),
             reads=["drv_t2", "drv"], writes=["drv"])

        def conv_w(key, n_m, src_fn):
            for m in range(n_m):
                i = rr["cv"]
                rr["cv"] = (i + 1) % 4
                S.dma("pool", lambda e, m=m: e.dma_start(out=ws[key][m], in_=src_fn(m)), ("wsd", i),
                      writes=[("ws", key, m)])

        def cols(wd):
            return lambda m: wd[:, m * 128:(m + 1) * 128].rearrange("(kc p) c -> p kc c", p=128)

        def gate_src(wd):
            return lambda m: wd[m // 2][:, (m % 2) * 128:(m % 2 + 1) * 128].rearrange("(kc p) c -> p kc c", p=128)

        conv_w("in", 64, cols(w_in))
        conv_w("a", 8, gate_src(w_a))
        conv_w("x", 8, gate_src(w_x))
        conv_w("br", 8, cols(w_br))
        conv_w("bl", 8, cols(w_bl))
        conv_w("o", 8, cols(w_o))
        conv_w("fi", 44, cols(w_fi))
        conv_w("fo", 8, cols(w_fo))

        def load_w(key, m, kc0, n):
            s = rr["w"]
            rr["w"] = (s + 1) % NWS
            S.dma("sp", lambda e: e.dma_start(out=wsl[s][:, 0:n, :], in_=ws[key][m][:, kc0:kc0 + n, :]), ("wl", s),
                  reads=[("ws", key, m)], writes=[("w", s)])
            return s

        def mm(out_ap, pairs, reads, writes):
            def fn(e):
                n = len(pairs)
                ins = None
                for i, (l, r) in enumerate(pairs):
                    ins = e.matmul(out_ap, l, r, start=(i == 0), stop=(i == n - 1))
                return ins
            S.op("pe", fn, reads=reads, writes=writes)

        def proj_fm(key, m, rhs_fn, rhs_res, nkc, nt):
            slots = []
            kc0 = 0
            while kc0 < nkc:
                n = min(8, nkc - kc0)
                slots.append((load_w(key, m, kc0, n), kc0, n))
                kc0 += n
            b = bank()
            pairs = []
            for (s, k0, n) in slots:
                for j in range(n):
                    pairs.append((wsl[s][:, j, :], rhs_fn(k0 + j)))
            mm(ps[b][:, 0:nt], pairs, reads=[("w", s) for (s, _, _) in slots] + rhs_res, writes=[("ps", b)])
            return b

        def rmsnorm(nt, wcol, dst_fn, dst_res_fn):
            for fc in range(8):
                S.op("act", lambda e, fc=fc: e.activation(out=B2[:, fc, 0:nt], in_=h[:, fc, 0:nt], func=AF.Square),
                     reads=[("h", fc)], writes=[("B2", fc)])
            b = bank()
            mm(ps[b][:, 0:nt], [(ones_bf[:, :], B2[:, fc, 0:nt]) for fc in range(8)],
               reads=["ones_bf"] + [("B2", fc) for fc in range(8)], writes=[("ps", b)])
            t1 = tslot()
            S.op("act", lambda e: e.activation(out=tmp[t1][:, 0:nt], in_=ps[b][:, 0:nt], func=AF.Sqrt, scale=1.0 / D, bias=EPS),
                 reads=[("ps", b)], writes=[("tmp", t1)])
            S.op("dve", lambda e: e.reciprocal(out=tmp[t1][:, 0:nt], in_=tmp[t1][:, 0:nt]),
                 reads=[("tmp", t1)], writes=[("tmp", t1)])
            for fc in range(8):
                S.op("dve", lambda e, fc=fc: e.scalar_tensor_tensor(
                    out=dst_fn(fc), in0=h[:, fc, 0:nt], scalar=pv[:, wcol + fc:wcol + fc + 1], in1=tmp[t1][:, 0:nt],
                    op0=ALU.mult, op1=ALU.mult),
                    reads=[("h", fc), ("tmp", t1), "pv"], writes=[dst_res_fn(fc)])

        u_res = lambda fc: ("u", fc)
        u_all = [("u", fc) for fc in range(8)]

        def tile(ti):
            is_meta = ti < 0
            nt = NMETA if is_meta else NT
            nch = 1 if is_meta else NT // 128
            cs = NMETA if is_meta else 128
            pos0 = 0 if is_meta else NMETA + ti * NT

            S.dma("sp", lambda e: e.dma_start(out=cs_t[:, :, 0:nt], in_=cs_d[:, :, pos0:pos0 + nt]), "cs_t",
                  writes=["cs_t"])
            for tc in range(nch):
                xb = tc % 2
                if is_meta:
                    S.dma("sp", lambda e: e.dma_start(out=xin[0][0:NMETA, :], in_=meta), ("xin", 0), writes=[("xin", 0)])
                else:
                    r0 = ti * NT + tc * 128
                    S.dma("sp", lambda e, r0=r0, xb=xb: e.dma_start(out=xin[xb][:, :], in_=x[r0:r0 + 128, :]), ("xin", xb),
                          writes=[("xin", xb)])
                for half in range(2):
                    b = bank()

                    def fn(e, half=half, b=b, xb=xb):
                        ins = None
                        for j in range(4):
                            fc = half * 4 + j
                            ins = e.transpose(out=ps[b][:, j * 128:j * 128 + cs], in_=xin[xb][0:cs, fc * 128:(fc + 1) * 128],
                                              identity=ident[0:cs, 0:cs])
                        return ins
                    S.op("pe", fn, reads=[("xin", xb), "ident"], writes=[("ps", b)])
                    src = ps[b][:, :].rearrange("p (a c) -> p a c", a=4)[:, :, 0:cs]
                    dst = h[:, half * 4:(half + 1) * 4, tc * 128:tc * 128 + cs]
                    eng = "act" if half == 0 else "dve"
                    if eng == "act":
                        S.op("act", lambda e, src=src, dst=dst: e.activation(out=dst, in_=src, func=AF.Copy),
                             reads=[("ps", b)], writes=[("h", half * 4 + j) for j in range(4)])
                    else:
                        S.op("dve", lambda e, src=src, dst=dst: e.tensor_copy(out=dst, in_=src),
                             reads=[("ps", b)], writes=[("h", half * 4 + j) for j in range(4)])

            rmsnorm(nt, 0, lambda fc: u[:, fc, 0:nt], u_res)
            u_rhs = lambda kc: u[:, kc, 0:nt]

            def rot(m_glob, hc, dst_res):
                b1 = proj_fm("in", m_glob, u_rhs, u_all, 8, nt)
                qb = bslot()
                S.op("act", lambda e: e.activation(out=tmb[qb][:, 0:nt], in_=ps[b1][:, 0:nt], func=AF.Copy),
                     reads=[("ps", b1)], writes=[("tmb", qb)])
                qc = tslot()
                S.op("dve", lambda e: e.tensor_tensor(out=tmp[qc][:, 0:nt], in0=ps[b1][:, 0:nt], in1=cs_t[:, 0, 0:nt], op=ALU.mult),
                     reads=[("ps", b1), "cs_t"], writes=[("tmp", qc)])
                b2 = bank()
                mm(ps[b2][:, 0:nt], [(swap_bf[:, :], tmb[qb][:, 0:nt])], reads=["swap_bf", ("tmb", qb)], writes=[("ps", b2)])
                qs = tslot()
                S.op("dve", lambda e: e.tensor_tensor(out=tmp[qs][:, 0:nt], in0=ps[b2][:, 0:nt], in1=cs_t[:, 1, 0:nt], op=ALU.mult),
                     reads=[("ps", b2), "cs_t"], writes=[("tmp", qs)])
                return qc, qs

            if not is_meta:
                for hc in range(8):
                    qc, qs = rot(hc, hc, None)
                    S.op("pool", lambda e, hc=hc, qc=qc, qs=qs: e.tensor_tensor(out=BB[:, hc, 0:nt], in0=tmp[qc][:, 0:nt],
                                                                                in1=tmp[qs][:, 0:nt], op=ALU.add),
                         reads=[("tmp", qc), ("tmp", qs)], writes=[("BB", hc)])

                    def fn(e, hc=hc):
                        ins = None
                        for tc in range(nch):
                            ins = e.tensor_tensor(out=B2[:, hc, tc * 128:(tc + 1) * 128], in0=BB[:, hc, tc * 128:(tc + 1) * 128],
                                                  in1=qd[:, hc, :], op=ALU.mult)
                        return ins
                    S.op("pool", fn, reads=[("BB", hc), "qd"], writes=[("B2", hc)])
            for hc in range(8):
                qc, qs = rot(8 + hc, hc, None)
                S.op("pool", lambda e, hc=hc, qc=qc, qs=qs: e.tensor_tensor(out=BB[:, 8 + hc, 0:nt], in0=tmp[qc][:, 0:nt],
                                                                            in1=tmp[qs][:, 0:nt], op=ALU.add),
                     reads=[("tmp", qc), ("tmp", qs)], writes=[("BB", 8 + hc)])
            kcol = 8 if is_meta else 0
            for tc in range(nch):
                for half in range(2):
                    b = bank()
                    mm_pairs = []

                    def fn(e, tc=tc, half=half, b=b):
                        ins = None
                        for j in range(4):
                            hc = half * 4 + j
                            ins = e.matmul(ps[b][0:cs, j * 128:(j + 1) * 128], BB[:, 8 + hc, tc * 128:tc * 128 + cs],
                                           ident_bf[:, :], start=True, stop=True)
                        return ins
                    S.op("pe", fn, reads=[("BB", 8 + half * 4 + j) for j in range(4)] + ["ident_bf"], writes=[("ps", b)])

                    def fn2(e, tc=tc, half=half, b=b):
                        ins = None
                        for j in range(4):
                            hc = half * 4 + j
                            ins = e.activation(out=k_dec[0:cs, tc, hc, :], in_=ps[b][0:cs, j * 128:(j + 1) * 128], func=AF.Identity,
                                               scale=kd[0:cs, kcol + hc:kcol + hc + 1])
                        return ins
                    S.op("act", fn2, reads=[("ps", b), "kd"], writes=[("k_dec", tc, half)])

            for m in range(8):
                s = load_w("in", 16 + m, 0, 8)
                b = bank()

                def fn(e, s=s, b=b):
                    ins = None
                    for tc in range(nch):
                        for kc in range(8):
                            ins = e.matmul(ps[b][0:cs, tc * 128:(tc + 1) * 128], u[:, kc, tc * 128:tc * 128 + cs], wsl[s][:, kc, :],
                                           start=(kc == 0), stop=(kc == 7))
                    return ins
                S.op("pe", fn, reads=[("w", s)] + u_all, writes=[("ps", b)])
                src = ps[b][0:cs, 0:nch * 128].rearrange("p (a c) -> p a c", a=nch)
                dst = v_tm[0:cs, 0:nch, m * 128:(m + 1) * 128]
                if m % 2 == 0:
                    S.op("act", lambda e, src=src, dst=dst: e.activation(out=dst, in_=src, func=AF.Copy),
                         reads=[("ps", b)], writes=[("v_tm", m)])
                else:
                    S.op("dve", lambda e, src=src, dst=dst: e.tensor_copy(out=dst, in_=src),
                         reads=[("ps", b)], writes=[("v_tm", m)])

            if is_meta:
                for half in range(2):
                    b = bank()

                    def fn(e, half=half, b=b):
                        ins = None
                        for j in range(4):
                            hc = half * 4 + j
                            ins = e.matmul(ps[b][:, j * 128:(j + 1) * 128], k_dec[0:cs, 0, hc, :], v_tm[0:cs, 0, hc * 128:(hc + 1) * 128],
                                           start=True, stop=True)
                        return ins
                    S.op("pe", fn, reads=[("k_dec", 0, half)] + [("v_tm", half * 4 + j) for j in range(4)], writes=[("ps", b)])
                    S.op("dve", lambda e, half=half, b=b: e.tensor_copy(
                        out=state[:, half * 4:(half + 1) * 4, :], in_=ps[b][:, :].rearrange("p (a c) -> p a c", a=4)),
                        reads=[("ps", b)], writes=[("state", half)])
                    S.op("act", lambda e, half=half, b=b: e.activation(
                        out=state_bf[:, half * 4:(half + 1) * 4, :], in_=ps[b][:, :].rearrange("p (a c) -> p a c", a=4), func=AF.Copy),
                        reads=[("ps", b)], writes=[("state_bf", half)])
            else:
                for m in range(8):
                    b = proj_fm("in", 24 + m, u_rhs, u_all, 8, nt)
                    S.op("act", lambda e, m=m, b=b: e.activation(out=BB[:, 16 + m, 0:nt], in_=ps[b][:, 0:nt], func=AF.Silu),
                         reads=[("ps", b)], writes=[("BB", 16 + m)])
                for tc in range(nch):
                    tsl = slice(tc * 128, (tc + 1) * 128)
                    sti = rr["st"]
                    rr["st"] = 1 - sti
                    for half in range(2):
                        b = bank()

                        def fn(e, half=half, b=b, tsl=tsl):
                            ins = None
                            for j in range(4):
                                hc = half * 4 + j
                                ins = e.matmul(ps[b][:, j * 128:(j + 1) * 128], BB[:, 8 + hc, tsl], BB[:, hc, tsl], start=True, stop=True)
                            return ins
                        S.op("pe", fn, reads=[("BB", 8 + half * 4 + j) for j in range(4)] + [("BB", half * 4 + j) for j in range(4)],
                             writes=[("ps", b)])
                        S.op("dve", lambda e, half=half, b=b, sti=sti: e.tensor_tensor(
                            out=stT[sti][:, half * 4:(half + 1) * 4, :], in0=ps[b][:, :].rearrange("p (a c) -> p a c", a=4),
                            in1=maskT[:, half * 4:(half + 1) * 4, :], op=ALU.mult),
                            reads=[("ps", b), "maskT"], writes=[("stT", sti, half)])
                    for half in range(2):
                        bo = bank()

                        def fn(e, half=half, bo=bo, tsl=tsl, tc=tc, sti=sti):
                            ins = None
                            for j in range(4):
                                hc = half * 4 + j
                                e.matmul(ps[bo][:, j * 128:(j + 1) * 128], v_tm[:, tc, hc * 128:(hc + 1) * 128], stT[sti][:, hc, :],
                                         start=True, stop=False)
                                ins = e.matmul(ps[bo][:, j * 128:(j + 1) * 128], state_bf[:, hc, :], B2[:, hc, tsl],
                                               start=False, stop=True)
                            return ins
                        S.op("pe", fn, reads=[("v_tm", half * 4 + j) for j in range(4)] + [("stT", sti, half), ("state_bf", half)]
                             + [("B2", half * 4 + j) for j in range(4)], writes=[("ps", bo)])
                        sq = bslot()
                        S.op("act", lambda e, bo=bo, sq=sq: e.activation(out=tmb[sq][:, :], in_=ps[bo][:, :], func=AF.Square),
                             reads=[("ps", bo)], writes=[("tmb", sq)])
                        bs = bank()
                        mm(ps[bs][:, :], [(ones_bf[:, :], tmb[sq][:, :])], reads=["ones_bf", ("tmb", sq)], writes=[("ps", bs)])
                        t1 = tslot()
                        S.op("act", lambda e, bs=bs, t1=t1: e.activation(out=tmp[t1][:, :], in_=ps[bs][:, :], func=AF.Sqrt,
                                                                         scale=1.0 / 128, bias=EPS),
                             reads=[("ps", bs)], writes=[("tmp", t1)])
                        S.op("dve", lambda e, t1=t1: e.reciprocal(out=tmp[t1][:, :], in_=tmp[t1][:, :]),
                             reads=[("tmp", t1)], writes=[("tmp", t1)])
                        S.op("pool", lambda e, t1=t1, half=half, tsl=tsl: e.tensor_tensor(
                            out=tmp[t1][:, :].rearrange("p (a c) -> p a c", a=4), in0=tmp[t1][:, :].rearrange("p (a c) -> p a c", a=4),
                            in1=BB[:, 16 + half * 4:16 + (half + 1) * 4, tsl], op=ALU.mult),
                            reads=[("tmp", t1)] + [("BB", 16 + half * 4 + j) for j in range(4)], writes=[("tmp", t1)])
                        S.op("dve", lambda e, t1=t1, half=half, tsl=tsl, bo=bo: e.tensor_tensor(
                            out=yg[:, half * 4:(half + 1) * 4, tsl], in0=ps[bo][:, :].rearrange("p (a c) -> p a c", a=4),
                            in1=tmp[t1][:, :].rearrange("p (a c) -> p a c", a=4), op=ALU.mult),
                            reads=[("ps", bo), ("tmp", t1)], writes=[("yg", half * 4 + j) for j in range(4)])
                    for half in range(2):
                        bk = bank()

                        def fn(e, half=half, bk=bk, tc=tc):
                            ins = None
                            for j in range(4):
                                hc = half * 4 + j
                                ins = e.matmul(ps[bk][:, j * 128:(j + 1) * 128], k_dec[:, tc, hc, :], v_tm[:, tc, hc * 128:(hc + 1) * 128],
                                               start=True, stop=True)
                            return ins
                        S.op("pe", fn, reads=[("k_dec", tc, half)] + [("v_tm", half * 4 + j) for j in range(4)], writes=[("ps", bk)])

                        def fn2(e, half=half, bk=bk):
                            ins = None
                            for j in range(4):
                                hc = half * 4 + j
                                ins = e.scalar_tensor_tensor(out=state[:, hc, :], in0=state[:, hc, :], scalar=float(CD[hc]),
                                                             in1=ps[bk][:, j * 128:(j + 1) * 128], op0=ALU.mult, op1=ALU.add)
                            return ins
                        S.op("dve", fn2, reads=[("ps", bk), ("state", half)], writes=[("state", half)])
                        S.op("act", lambda e, half=half: e.activation(out=state_bf[:, half * 4:(half + 1) * 4, :],
                                                                      in_=state[:, half * 4:(half + 1) * 4, :], func=AF.Copy),
                             reads=[("state", half)], writes=[("state_bf", half)])

            for g in range(4):
                for j in range(2):
                    m = 2 * g + j
                    b = proj_fm("in", 32 + m, u_rhs, u_all, 8, nt)
                    S.op("pool", lambda e, j=j, m=m: e.tensor_copy(out=lin[:, j, 0:3], in_=halo[:, m, :]),
                         reads=["halo"], writes=[("lin", j)])
                    S.op("act", lambda e, j=j, b=b: e.activation(out=lin[:, j, 3:3 + nt], in_=ps[b][:, 0:nt], func=AF.Copy),
                         reads=[("ps", b)], writes=[("lin", j)])
                    S.op("act", lambda e, j=j, m=m, b=b: e.activation(out=cc[:, j, 0:nt], in_=ps[b][:, 0:nt], func=AF.Identity,
                                                                      scale=pv[:, 24 + 3 * 8 + m:24 + 3 * 8 + m + 1],
                                                                      bias=pv[:, 56 + m:56 + m + 1]),
                         reads=[("ps", b), "pv"], writes=[("cc", j)])
                    for tap in range(3):
                        S.op("dve", lambda e, j=j, m=m, tap=tap: e.scalar_tensor_tensor(
                            out=cc[:, j, 0:nt], in0=lin[:, j, tap:tap + nt], scalar=pv[:, 24 + tap * 8 + m:24 + tap * 8 + m + 1],
                            in1=cc[:, j, 0:nt], op0=ALU.mult, op1=ALU.add),
                            reads=[("lin", j), ("cc", j), "pv"], writes=[("cc", j)])
                    S.op("pool", lambda e, j=j, m=m: e.tensor_copy(out=halo[:, m, :], in_=lin[:, j, nt:nt + 3]),
                         reads=[("lin", j)], writes=["halo"])
                    S.op("pool", lambda e, j=j: e.tensor_copy(out=BB[:, j, 0:nt], in_=cc[:, j, 0:nt]),
                         reads=[("cc", j)], writes=[("BB", j)])
                c_rhs = lambda kc: BB[:, kc, 0:nt]
                c_res = [("BB", 0), ("BB", 1)]
                for j in range(2):
                    m = 2 * g + j
                    br_ = proj_fm("a", m, c_rhs, c_res, 2, nt)
                    thr = tslot()
                    S.op("act", lambda e, thr=thr, br_=br_, m=m: e.activation(out=tmp[thr][:, 0:nt], in_=ps[br_][:, 0:nt], func=AF.Tanh,
                                                                              scale=0.5, bias=drv[:, m:m + 1]),
                         reads=[("ps", br_), "drv"], writes=[("tmp", thr)])
                    bi_ = proj_fm("x", m, c_rhs, c_res, 2, nt)
                    thi = tslot()
                    S.op("act", lambda e, thi=thi, bi_=bi_, m=m: e.activation(out=tmp[thi][:, 0:nt], in_=ps[bi_][:, 0:nt], func=AF.Tanh,
                                                                              scale=0.5, bias=drv[:, 8 + m:8 + m + 1]),
                         reads=[("ps", bi_), "drv"], writes=[("tmp", thi)])
                    S.op("act", lambda e, thr=thr, j=j, m=m: e.activation(out=la[:, j, 0:nt], in_=tmp[thr][:, 0:nt], func=AF.Exp,
                                                                          scale=drv[:, 24 + m:24 + m + 1], bias=drv[:, 24 + m:24 + m + 1]),
                         reads=[("tmp", thr), "drv"], writes=[("la", j)])
                    S.op("act", lambda e, thr=thr, j=j, m=m: e.activation(out=la2[:, j, 0:nt], in_=tmp[thr][:, 0:nt], func=AF.Exp,
                                                                          scale=drv[:, 16 + m:16 + m + 1], bias=drv[:, 16 + m:16 + m + 1]),
                         reads=[("tmp", thr), "drv"], writes=[("la2", j)])
                    S.op("dve", lambda e, thi=thi, j=j: e.scalar_tensor_tensor(
                        out=luu[:, j, 0:nt], in0=tmp[thi][:, 0:nt], scalar=1.0, in1=cc[:, j, 0:nt], op0=ALU.add, op1=ALU.mult),
                        reads=[("tmp", thi), ("cc", j)], writes=[("luu", j)])
                for j in range(2):
                    m = 2 * g + j
                    S.op("act", lambda e, j=j: e.activation(out=la2[:, j, 0:nt], in_=la2[:, j, 0:nt], func=AF.Sqrt, scale=-0.25, bias=0.25),
                         reads=[("la2", j)], writes=[("la2", j)])
                    S.op("dve", lambda e, j=j: e.tensor_tensor(out=luu[:, j, 0:nt], in0=luu[:, j, 0:nt], in1=la2[:, j, 0:nt], op=ALU.mult),
                         reads=[("luu", j), ("la2", j)], writes=[("luu", j)])
                    S.op("dve", lambda e, j=j, m=m: e.tensor_tensor_scan(out=la2[:, j, 0:nt], data0=la[:, j, 0:nt], data1=luu[:, j, 0:nt],
                                                                         initial=hst[:, m:m + 1], op0=ALU.mult, op1=ALU.add),
                         reads=[("la", j), ("luu", j), "hst", ("la2", j)], writes=[("la2", j)])
                    S.op("pool", lambda e, j=j, m=m: e.tensor_copy(out=hst[:, m:m + 1], in_=la2[:, j, nt - 1:nt]),
                         reads=[("la2", j)], writes=["hst"])
                    if not is_meta:
                        b = proj_fm("in", 40 + m, u_rhs, u_all, 8, nt)
                        gl = tslot()
                        S.op("act", lambda e, gl=gl, b=b: e.activation(out=tmp[gl][:, 0:nt], in_=ps[b][:, 0:nt], func=AF.Gelu_apprx_tanh),
                             reads=[("ps", b)], writes=[("tmp", gl)])
                        S.op("dve", lambda e, gl=gl, j=j, m=m: e.tensor_tensor(out=ylg[:, m, 0:nt], in0=tmp[gl][:, 0:nt], in1=la2[:, j, 0:nt],
                                                                               op=ALU.mult),
                             reads=[("tmp", gl), ("la2", j)], writes=[("ylg", m)])
            if is_meta:
                return

            yg_all = [("yg", k) for k in range(8)]
            ylg_all = [("ylg", k) for k in range(8)]
            for m in range(8):
                bA = proj_fm("br", m, lambda kc: yg[:, kc, 0:nt], yg_all, 8, nt)
                bC = proj_fm("in", 48 + m, u_rhs, u_all, 8, nt)
                tha = tslot()
                S.op("act", lambda e, tha=tha, bC=bC: e.activation(out=tmp[tha][:, 0:nt], in_=ps[bC][:, 0:nt], func=AF.Tanh, scale=0.5),
                     reads=[("ps", bC)], writes=[("tmp", tha)])
                S.op("dve", lambda e, tha=tha, bA=bA: e.scalar_tensor_tensor(out=tmp[tha][:, 0:nt], in0=tmp[tha][:, 0:nt], scalar=1.0,
                                                                             in1=ps[bA][:, 0:nt], op0=ALU.add, op1=ALU.mult),
                     reads=[("tmp", tha), ("ps", bA)], writes=[("tmp", tha)])
                bB = proj_fm("bl", m, lambda kc: ylg[:, kc, 0:nt], ylg_all, 8, nt)
                bD = proj_fm("in", 56 + m, u_rhs, u_all, 8, nt)
                thb = tslot()
                S.op("act", lambda e, thb=thb, bD=bD: e.activation(out=tmp[thb][:, 0:nt], in_=ps[bD][:, 0:nt], func=AF.Tanh, scale=0.5),
                     reads=[("ps", bD)], writes=[("tmp", thb)])
                S.op("dve", lambda e, thb=thb, bB=bB: e.scalar_tensor_tensor(out=tmp[thb][:, 0:nt], in0=tmp[thb][:, 0:nt], scalar=1.0,
                                                                             in1=ps[bB][:, 0:nt], op0=ALU.add, op1=ALU.mult),
                     reads=[("tmp", thb), ("ps", bB)], writes=[("tmp", thb)])
                S.op("pool", lambda e, tha=tha, thb=thb, m=m: e.tensor_tensor(out=BB[:, 8 + m, 0:nt], in0=tmp[tha][:, 0:nt],
                                                                              in1=tmp[thb][:, 0:nt], op=ALU.add),
                     reads=[("tmp", tha), ("tmp", thb)], writes=[("BB", 8 + m)])
            mx_all = [("BB", 8 + k) for k in range(8)]
            for m in range(8):
                b = proj_fm("o", m, lambda kc: BB[:, 8 + kc, 0:nt], mx_all, 8, nt)
                S.op("dve", lambda e, m=m, b=b: e.scalar_tensor_tensor(out=h[:, m, 0:nt], in0=ps[b][:, 0:nt], scalar=0.5, in1=h[:, m, 0:nt],
                                                                       op0=ALU.mult, op1=ALU.add),
                     reads=[("ps", b), ("h", m)], writes=[("h", m)])
            rmsnorm(nt, 8, lambda fc: u[:, fc, 0:nt], u_res)
            for jf in range(NFF):
                bG = proj_fm("fi", jf, u_rhs, u_all, 8, nt)
                sg_ = tslot()
                S.op("act", lambda e, sg_=sg_, bG=bG: e.activation(out=tmp[sg_][:, 0:nt], in_=ps[bG][:, 0:nt], func=AF.Silu),
                     reads=[("ps", bG)], writes=[("tmp", sg_)])
                bU = proj_fm("fi", NFF + jf, u_rhs, u_all, 8, nt)
                S.op("dve", lambda e, sg_=sg_, bU=bU, jf=jf: e.tensor_tensor(out=BB[:, jf, 0:nt], in0=tmp[sg_][:, 0:nt], in1=ps[bU][:, 0:nt],
                                                                             op=ALU.mult),
                     reads=[("tmp", sg_), ("ps", bU)], writes=[("BB", jf)])
            act_all = [("BB", k) for k in range(NFF)]
            for m in range(8):
                b = proj_fm("fo", m, lambda kc: BB[:, kc, 0:nt], act_all, NFF, nt)
                S.op("dve", lambda e, m=m, b=b: e.tensor_tensor(out=h[:, m, 0:nt], in0=ps[b][:, 0:nt], in1=h[:, m, 0:nt], op=ALU.add),
                     reads=[("ps", b), ("h", m)], writes=[("h", m)])
            rmsnorm(nt, 16, lambda fc: h[:, fc, 0:nt], lambda fc: ("h", fc))
            for tc in range(nch):
                ob = tc % 2
                for half in range(2):
                    b = bank()

                    def fn(e, half=half, b=b, tc=tc):
                        ins = None
                        for j in range(4):
                            fc = half * 4 + j
                            ins = e.transpose(out=ps[b][:, j * 128:(j + 1) * 128], in_=h[:, fc, tc * 128:(tc + 1) * 128], identity=ident[:, :])
                        return ins
                    S.op("pe", fn, reads=[("h", half * 4 + j) for j in range(4)] + ["ident"], writes=[("ps", b)])
                    if half == 0:
                        S.op("act", lambda e, b=b, ob=ob: e.activation(out=otm[ob][:, 0:512], in_=ps[b][:, :], func=AF.Copy),
                             reads=[("ps", b)], writes=[("otm", ob, 0)])
                    else:
                        S.op("dve", lambda e, b=b, ob=ob: e.tensor_copy(out=otm[ob][:, 512:1024], in_=ps[b][:, :]),
                             reads=[("ps", b)], writes=[("otm", ob, 1)])
                r0 = ti * NT + tc * 128
                S.dma("sp", lambda e, r0=r0, ob=ob: e.dma_start(out=out[r0:r0 + 128, :], in_=otm[ob][:, :]), ("ost", ob),
                      reads=[("otm", ob, 0), ("otm", ob, 1)], writes=[("out", ob)])

        log_g = np.log(1.0 - 2.0 ** (-5.0 - np.arange(8, dtype=np.float64)))
        CD = np.exp(128.0 * log_g)

        import os
        stage = int(os.environ.get("KSTAGE", "99"))
        if stage >= 1:
            tile(-1)
        if stage >= 2:
            for ti in range(n_tiles):
                tile(ti)
        S.final_wait("sp")
        with nc.Block() as block:
            S.emit(block)
    return nc


def host_consts(T):
    inv = (np.float32(10000.0) ** (-(np.arange(0, 128, 2, dtype=np.float32)) / np.float32(128))).astype(np.float32)
    pos = np.arange(T, dtype=np.float32)
    ang = (pos[None, :] * inv[:, None]).astype(np.float32).astype(np.float64)
    cos, sin = np.cos(ang), np.sin(ang)
    cs = np.stack([np.concatenate([cos, cos], 0), np.concatenate([-sin, sin], 0)], 1).astype(np.float32)
    log_g = np.log(1.0 - 2.0 ** (-5.0 - np.arange(8, dtype=np.float64)))
    idx = np.arange(128)
    diff = idx[None, :] - idx[:, None]
    maskT = np.where(diff[None] >= 0, np.exp(np.maximum(diff, 0)[None] * log_g[:, None, None]), 0.0) * 128.0 ** -0.5
    maskT = np.ascontiguousarray(maskT.transpose(1, 0, 2)).astype(np.float32)
    qd = np.exp((idx + 1.0)[None, :] * log_g[:, None])
    qd = np.ascontiguousarray(np.broadcast_to(qd[None], (128, 8, 128))).astype(np.float32)
    kd = np.zeros((128, 16), np.float64)
    kd[:, 0:8] = np.exp((127.0 - idx)[:, None] * log_g[None, :]) * 128.0 ** -0.5
    kd[:16, 8:16] = np.exp((15.0 - np.arange(16))[:, None] * log_g[None, :]) * 128.0 ** -0.5
    ident = np.eye(128, dtype=np.float32)
    swapP = np.zeros((128, 128), np.float32)
    swapP[(np.arange(128) + 64) % 128, np.arange(128)] = 1.0
    return dict(cs_tab=cs, maskT=maskT, qd=qd, kd=kd.astype(np.float32), ident=ident, swapP=swapP)


def col_layout(v):
    return np.ascontiguousarray(np.asarray(v, np.float32).reshape(8, 128).T)


_NC_CACHE = {}


def run(inputs, n_tiles, n_cores):
    x = np.asarray(inputs["x"], np.float32)
    T = NMETA + n_tiles * NT
    consts = host_consts(T)
    cw = np.asarray(inputs["conv_w"], np.float32)[0]
    pv = np.concatenate(
        [col_layout(inputs["mix_norm_w"][0]), col_layout(inputs["ffn_norm_w"][0]), col_layout(inputs["final_norm_w"])]
        + [col_layout(cw[j]) for j in range(4)]
        + [col_layout(inputs["conv_b"][0]), col_layout(inputs["lru_ba"][0]), col_layout(inputs["lru_bx"][0]),
           col_layout(inputs["lru_lambda"][0])], axis=1)
    assert pv.shape == (128, 88)
    shared = dict(
        meta_tokens=np.asarray(inputs["meta_tokens"], np.float32),
        w_in=np.asarray(inputs["w_in"], np.float32)[0],
        w_branch_ret=np.asarray(inputs["w_branch_ret"], np.float32)[0],
        w_branch_lru=np.asarray(inputs["w_branch_lru"], np.float32)[0],
        w_out=np.asarray(inputs["w_out"], np.float32)[0],
        lru_wa=np.asarray(inputs["lru_wa"], np.float32)[0],
        lru_wx=np.asarray(inputs["lru_wx"], np.float32)[0],
        w_ffn_in=np.asarray(inputs["w_ffn_in"], np.float32)[0],
        w_ffn_out=np.asarray(inputs["w_ffn_out"], np.float32)[0],
        pv=np.ascontiguousarray(pv), **consts)
    if n_tiles not in _NC_CACHE:
        _NC_CACHE[n_tiles] = build(n_tiles)
    nc = _NC_CACHE[n_tiles]
    in_maps = [dict(shared, x=np.ascontiguousarray(x[b, :n_tiles * NT])) for b in range(n_cores)]
    res = run_bass_kernel_spmd(nc, in_maps, core_ids=list(range(n_cores)))
    return np.stack([np.asarray(res.results[b]["out"]) for b in range(n_cores)], 0).astype(np.float32)


def kernel(**inputs):
    x = inputs["x"]
    return run(inputs, x.shape[1] // NT, x.shape[0])
```

```python
import contextlib
import numpy as np
import concourse.bass as bass
import concourse.mybir as mybir
from concourse.bass_utils import run_bass_kernel_spmd

F32 = mybir.dt.float32
BF16 = mybir.dt.bfloat16
ALU = mybir.AluOpType
AF = mybir.ActivationFunctionType

D = 1024
NMETA = 16
NH = 8
FF = 2816
NFF = FF // 128
NT = 512
EPS = 1e-6
SEM_LIMIT = 30000
NWS = 8
NTMP = 10
NTB = 3
NPS = 8


class Counter:
    def __init__(self, sched, step):
        self.sched = sched
        self.step = step
        self.sems = [sched.new_sem()]
        self.val = 0

    def next_token(self):
        if self.val + self.step > SEM_LIMIT:
            self.sems.append(self.sched.new_sem())
            self.val = 0
        self.val += self.step
        return (self, len(self.sems) - 1, self.val)


class Sched:
    ENGS = ("pe", "act", "dve", "pool", "sp")

    def __init__(self, nc, stack, n_sems):
        self.nc = nc
        self.free_sems = [stack.enter_context(nc.semaphore(f"s{i}")) for i in range(n_sems)]
        self.streams = {e: [] for e in self.ENGS}
        self.ectr = {e: Counter(self, 1) for e in self.ENGS}
        self.known = {e: {} for e in self.ENGS}
        self.last_w = {}
        self.readers = {}
        self.dma_ctr = {}

    def new_sem(self):
        return self.free_sems.pop()

    def _deps(self, eng, reads, writes):
        deps = []
        for r in reads:
            t = self.last_w.get(r)
            if t is not None:
                deps.append(t)
            if isinstance(r, tuple) and r[0] == "ps":
                for t in self.readers.get(r, {}).values():
                    if t[0] is not self.ectr[eng]:
                        deps.append(t)
        for w in writes:
            t = self.last_w.get(w)
            if t is not None:
                deps.append(t)
            for t in self.readers.get(w, {}).values():
                if t[0] is self.ectr[eng]:
                    continue
                deps.append(t)
        best = {}
        for (c, ep, v) in deps:
            cur = best.get(c)
            if cur is None or (ep, v) > cur:
                best[c] = (ep, v)
        out = []
        kn = self.known[eng]
        for c, (ep, v) in best.items():
            cur = kn.get(c)
            if cur is not None and cur >= (ep, v):
                continue
            kn[c] = (ep, v)
            out.append((c.sems[ep], v))
        return out

    def _commit(self, tok, reads, writes):
        for r in reads:
            self.readers.setdefault(r, {})[tok[0]] = tok
        for w in writes:
            self.last_w[w] = tok
            self.readers[w] = {}

    def _skip(self, eng, fn):
        import os
        self.nops = getattr(self, "nops", 0) + 1
        mx = int(os.environ.get("KMAXOPS", "0"))
        if mx and self.nops == mx:
            print("LAST OP", self.nops, eng, "line", fn.__code__.co_firstlineno)
        return bool(mx) and self.nops > mx

    def op(self, eng, fn, reads=(), writes=()):
        if self._skip(eng, fn):
            return None
        waits = self._deps(eng, reads, writes)
        tok = self.ectr[eng].next_token()
        self.streams[eng].append((waits, fn, (tok[0].sems[tok[1]], 1)))
        self._commit(tok, reads, writes)
        return tok

    def dma(self, eng, fn, key, reads=(), writes=()):
        if self._skip(eng, fn):
            return None
        writes = list(writes) + [("dmakey", key)]
        waits = self._deps(eng, reads, writes)
        ctr = self.dma_ctr.get(key)
        if ctr is None:
            ctr = self.dma_ctr[key] = Counter(self, 16)
        tok = ctr.next_token()
        self.streams[eng].append((waits, fn, (tok[0].sems[tok[1]], 16)))
        self._commit(tok, reads, writes)
        return tok

    def final_wait(self, eng):
        waits = []
        for c in list(self.ectr.values()) + list(self.dma_ctr.values()):
            if c.val > 0:
                waits.append((c.sems[-1], c.val))
        self.streams[eng].append((waits, None, None))

    def emit(self, block):
        attr = {"pe": "tensor", "act": "scalar", "dve": "vector", "pool": "gpsimd", "sp": "sync"}
        for e in self.ENGS:
            stream = self.streams[e]

            def body(engine, stream=stream):
                for waits, fn, inc in stream:
                    for (sem, v) in waits:
                        engine.wait_ge(sem, v)
                    if fn is not None:
                        ins = fn(engine)
                        ins.then_inc(inc[0], inc[1])
            getattr(block, attr[e])(body)


def build(n_tiles):
    nc = bass.Bass("TRN2", target_bir_lowering=False)
    T = NMETA + n_tiles * NT

    def din(name, shape):
        return nc.dram_tensor(name, shape, F32, kind="ExternalInput").ap()

    x = din("x", [n_tiles * NT, D])
    meta = din("meta_tokens", [NMETA, D])
    w_in = din("w_in", [D, 8 * D])
    w_br = din("w_branch_ret", [D, D])
    w_bl = din("w_branch_lru", [D, D])
    w_o = din("w_out", [D, D])
    w_a = din("lru_wa", [4, 256, 256])
    w_x = din("lru_wx", [4, 256, 256])
    w_fi = din("w_ffn_in", [D, 2 * FF])
    w_fo = din("w_ffn_out", [FF, D])
    pv_d = din("pv", [128, 88])
    cs_d = din("cs_tab", [128, 2, T])
    mask_d = din("maskT", [128, 8, 128])
    qd_d = din("qd", [128, 8, 128])
    kd_d = din("kd", [128, 16])
    ident_d = din("ident", [128, 128])
    swap_d = din("swapP", [128, 128])
    out = nc.dram_tensor("out", [n_tiles * NT, D], F32, kind="ExternalOutput").ap()

    def dscr(name, shape):
        return nc.dram_tensor(name, shape, BF16, kind="Internal").ap()

    ws = {
        "in": dscr("ws_in", [64, 128, 8, 128]),
        "br": dscr("ws_br", [8, 128, 8, 128]),
        "bl": dscr("ws_bl", [8, 128, 8, 128]),
        "o": dscr("ws_o", [8, 128, 8, 128]),
        "a": dscr("ws_a", [8, 128, 2, 128]),
        "x": dscr("ws_x", [8, 128, 2, 128]),
        "fi": dscr("ws_fi", [44, 128, 8, 128]),
        "fo": dscr("ws_fo", [8, 128, NFF, 128]),
    }

    with contextlib.ExitStack() as st:
        S = Sched(nc, st, n_sems=96)

        def sb(name, shape, dt=F32):
            return st.enter_context(nc.sbuf_tensor(name, shape, dt))

        pv = sb("pv_s", [128, 96])
        ident = sb("ident_s", [128, 128])
        ident_bf = sb("ident_bf", [128, 128], BF16)
        ones_bf = sb("ones_bf", [128, 128], BF16)
        swap_bf = sb("swap_bf", [128, 128], BF16)
        maskT = sb("maskT_s", [128, 8, 128])
        qd = sb("qd_s", [128, 8, 128])
        kd = sb("kd_s", [128, 16])
        drv = sb("drv", [128, 48])
        state = sb("state", [128, 8, 128])
        state_bf = [sb(f"state_bf{i}", [128, 8, 128], BF16) for i in range(2)]
        hst = sb("hst", [128, 8])
        halo = sb("halo", [128, 8, 3])
        xin = [sb(f"xin{i}", [128, D]) for i in range(2)]
        otm = [sb(f"otm{i}", [128, D]) for i in range(2)]
        hh = [sb(f"h{i}", [128, 8, NT]) for i in range(2)]
        u = sb("u", [128, 8, NT], BF16)
        cs_t = sb("cs_t", [128, 2, NT])
        BB = sb("BB", [128, 24, NT], BF16)
        B2 = sb("B2", [128, 8, NT], BF16)
        k_dec = sb("k_dec", [128, 4, 8, 128], BF16)
        v_tm = sb("v_tm", [128, 4, D], BF16)
        yg = sb("yg", [128, 8, NT], BF16)
        ylg = sb("ylg", [128, 8, NT], BF16)
        lin = sb("lin", [128, 2, NT + 3])
        cc = sb("cc", [128, 2, NT])
        cbf = sb("cbf", [128, 2, NT], BF16)
        la = sb("la", [128, 2, NT])
        la2 = sb("la2", [128, 2, NT])
        luu = sb("luu", [128, 2, NT])
        stT = [sb(f"stT{i}", [128, 8, 128], BF16) for i in range(2)]
        tmp = [sb(f"tmp{i}", [128, NT]) for i in range(NTMP)]
        tmb = [sb(f"tmb{i}", [128, NT], BF16) for i in range(NTB)]
        wsl = [sb(f"wsl{i}", [128, 8, 128], BF16) for i in range(NWS)]
        ps = [st.enter_context(nc.psum_tensor(f"ps{i}", [128, 512], F32)) for i in range(NPS)]

        rr = {"ps": 0, "tmp": 0, "tmb": 0, "w": 0, "st": 0, "cv": 0}

        def bank():
            b = rr["ps"]
            rr["ps"] = (b + 1) % NPS
            return b

        def tslot():
            b = rr["tmp"]
            rr["tmp"] = (b + 1) % NTMP
            return b

        def bslot():
            b = rr["tmb"]
            rr["tmb"] = (b + 1) % NTB
            return b

        def ld(dst, src, key):
            S.dma("sp", lambda e: e.dma_start(out=dst, in_=src), key, writes=[key])

        ld(pv[:, 0:88], pv_d, "pv")
        ld(ident[:], ident_d, "ident")
        ld(maskT[:], mask_d, "maskT")
        ld(qd[:], qd_d, "qd")
        ld(kd[:], kd_d, "kd")
        S.dma("pool", lambda e: e.dma_start(out=ident_bf[:], in_=ident_d), "ident_bf", writes=["ident_bf"])
        S.dma("pool", lambda e: e.dma_start(out=swap_bf[:], in_=swap_d), "swap_bf", writes=["swap_bf"])
        S.op("pool", lambda e: e.memset(ones_bf[:], 1.0), writes=["ones_bf"])
        S.op("pool", lambda e: e.memset(halo[:], 0.0), writes=["halo"])
        S.op("pool", lambda e: e.memset(hst[:], 0.0), writes=["hst"])
        S.op("dve", lambda e: e.tensor_scalar(out=drv[:, 0:16], in0=pv[:, 64:80], scalar1=0.5, scalar2=None, op0=ALU.mult),
             reads=["pv"], writes=["drv"])
        S.op("act", lambda e: e.activation(out=drv[:, 32:40], in_=pv[:, 80:88], func=AF.Exp, scale=-1.0),
             reads=["pv"], writes=["drv_t"])
        S.op("act", lambda e: e.activation(out=drv[:, 40:48], in_=drv[:, 32:40], func=AF.Ln, bias=1.0),
             reads=["drv_t"], writes=["drv_t2"])
        S.op("dve", lambda e: e.tensor_scalar(out=drv[:, 16:24], in0=drv[:, 40:48], scalar1=-8.0, scalar2=None, op0=ALU.mult),
             reads=["drv_t2", "drv"], writes=["drv"])
        S.op("dve", lambda e: e.tensor_scalar(out=drv[:, 24:32], in0=drv[:, 40:48], scalar1=-4.0, scalar2=None, op0=ALU.mult),
             reads=["drv_t2", "drv"], writes=["drv"])

        def conv_w(key, n_m, src_fn):
            for m in range(n_m):
                i = rr["cv"]
                rr["cv"] = (i + 1) % 4
                S.dma("pool", lambda e, m=m: e.dma_start(out=ws[key][m], in_=src_fn(m)), ("wsd", i),
                      writes=[("ws", key, m)])

        def cols(wd):
            return lambda m: wd[:, m * 128:(m + 1) * 128].rearrange("(kc p) c -> p kc c", p=128)

        def gate_src(wd):
            return lambda m: wd[m // 2][:, (m % 2) * 128:(m % 2 + 1) * 128].rearrange("(kc p) c -> p kc c", p=128)

        conv_w("in", 64, cols(w_in))
        conv_w("a", 8, gate_src(w_a))
        conv_w("x", 8, gate_src(w_x))
        conv_w("br", 8, cols(w_br))
        conv_w("bl", 8, cols(w_bl))
        conv_w("o", 8, cols(w_o))
        conv_w("fi", 44, cols(w_fi))
        conv_w("fo", 8, cols(w_fo))

        def load_w(key, m, kc0, n):
            s = rr["w"]
            rr["w"] = (s + 1) % NWS
            S.dma("sp", lambda e: e.dma_start(out=wsl[s][:, 0:n, :], in_=ws[key][m][:, kc0:kc0 + n, :]), ("wl", s),
                  reads=[("ws", key, m)], writes=[("w", s)])
            return s

        def mm(out_ap, pairs, reads, writes):
            def fn(e):
                n = len(pairs)
                ins = None
                for i, (l, r) in enumerate(pairs):
                    ins = e.matmul(out_ap, l, r, start=(i == 0), stop=(i == n - 1))
                return ins
            S.op("pe", fn, reads=reads, writes=writes)

        def proj_fm(key, m, rhs_fn, rhs_res, nkc, nt):
            slots = []
            kc0 = 0
            while kc0 < nkc:
                n = min(8, nkc - kc0)
                slots.append((load_w(key, m, kc0, n), kc0, n))
                kc0 += n
            b = bank()
            pairs = []
            for (s, k0, n) in slots:
                for j in range(n):
                    pairs.append((wsl[s][:, j, :], rhs_fn(k0 + j)))
            mm(ps[b][:, 0:nt], pairs, reads=[("w", s) for (s, _, _) in slots] + rhs_res, writes=[("ps", b)])
            return b

        u_res = lambda fc: ("u", fc)
        u_all = [("u", fc) for fc in range(8)]
        yg_all = [("yg", k) for k in range(8)]
        ylg_all = [("ylg", k) for k in range(8)]

        def rmsnorm(hb, hk, nt, wcol, scr, scr_key, dst_fn, dst_res_fn):
            for fc in range(8):
                S.op("act", lambda e, fc=fc: e.activation(out=scr[:, fc, 0:nt], in_=hb[:, fc, 0:nt], func=AF.Square),
                     reads=[(hk, fc)], writes=[(scr_key, fc)])
            yield
            b = bank()
            mm(ps[b][:, 0:nt], [(ones_bf[:, :], scr[:, fc, 0:nt]) for fc in range(8)],
               reads=["ones_bf"] + [(scr_key, fc) for fc in range(8)], writes=[("ps", b)])
            t1 = tslot()
            S.op("act", lambda e: e.activation(out=tmp[t1][:, 0:nt], in_=ps[b][:, 0:nt], func=AF.Ln, scale=1.0 / D, bias=EPS),
                 reads=[("ps", b)], writes=[("tmp", t1)])
            S.op("act", lambda e: e.activation(out=tmp[t1][:, 0:nt], in_=tmp[t1][:, 0:nt], func=AF.Exp, scale=-0.5),
                 reads=[("tmp", t1)], writes=[("tmp", t1)])
            yield
            for fc in range(8):
                S.op("dve", lambda e, fc=fc: e.scalar_tensor_tensor(
                    out=dst_fn(fc), in0=hb[:, fc, 0:nt], scalar=pv[:, wcol + fc:wcol + fc + 1], in1=tmp[t1][:, 0:nt],
                    op0=ALU.mult, op1=ALU.mult),
                    reads=[(hk, fc), ("tmp", t1), "pv"], writes=[dst_res_fn(fc)])
                if fc % 4 == 3:
                    yield

        def roundrobin(gens):
            gens = [g for g in gens if g is not None]
            while gens:
                for g in list(gens):
                    try:
                        next(g)
                        yield
                    except StopIteration:
                        gens.remove(g)

        def tile_head(ti):
            is_meta = ti < 0
            nt = NMETA if is_meta else NT
            nch = 1 if is_meta else NT // 128
            cs = NMETA if is_meta else 128
            pos0 = 0 if is_meta else NMETA + ti * NT
            hb = hh[ti % 2]
            hk = "h%d" % (ti % 2)
            u_rhs = lambda kc: u[:, kc, 0:nt]

            S.dma("sp", lambda e: e.dma_start(out=cs_t[:, :, 0:nt], in_=cs_d[:, :, pos0:pos0 + nt]), "cs_t",
                  writes=["cs_t"])
            for tc in range(nch):
                xb = tc % 2
                if is_meta:
                    S.dma("sp", lambda e: e.dma_start(out=xin[0][0:NMETA, :], in_=meta), ("xin", 0), writes=[("xin", 0)])
                else:
                    r0 = ti * NT + tc * 128
                    S.dma("sp", lambda e, r0=r0, xb=xb: e.dma_start(out=xin[xb][:, :], in_=x[r0:r0 + 128, :]), ("xin", xb),
                          writes=[("xin", xb)])
                for half in range(2):
                    b = bank()

                    def fn(e, half=half, b=b, xb=xb):
                        ins = None
                        for j in range(4):
                            fc = half * 4 + j
                            ins = e.transpose(out=ps[b][:, j * 128:j * 128 + cs], in_=xin[xb][0:cs, fc * 128:(fc + 1) * 128],
                                              identity=ident[0:cs, 0:cs])
                        return ins
                    S.op("pe", fn, reads=[("xin", xb), "ident"], writes=[("ps", b)])
                    src = ps[b][:, :].rearrange("p (a c) -> p a c", a=4)[:, :, 0:cs]
                    dst = hb[:, half * 4:(half + 1) * 4, tc * 128:tc * 128 + cs]
                    if half == 0:
                        S.op("act", lambda e, src=src, dst=dst: e.activation(out=dst, in_=src, func=AF.Copy),
                             reads=[("ps", b)], writes=[(hk, half * 4 + j) for j in range(4)])
                    else:
                        S.op("dve", lambda e, src=src, dst=dst: e.tensor_copy(out=dst, in_=src),
                             reads=[("ps", b)], writes=[(hk, half * 4 + j) for j in range(4)])
                yield

            yield from rmsnorm(hb, hk, nt, 0, u, "u", lambda fc: u[:, fc, 0:nt], u_res)

            def rot_a(m_glob):
                b1 = proj_fm("in", m_glob, u_rhs, u_all, 8, nt)
                qb = bslot()
                S.op("act", lambda e: e.activation(out=tmb[qb][:, 0:nt], in_=ps[b1][:, 0:nt], func=AF.Copy),
                     reads=[("ps", b1)], writes=[("tmb", qb)])
                qc = tslot()
                S.op("dve", lambda e: e.tensor_tensor(out=tmp[qc][:, 0:nt], in0=tmb[qb][:, 0:nt], in1=cs_t[:, 0, 0:nt], op=ALU.mult),
                     reads=[("tmb", qb), "cs_t"], writes=[("tmp", qc)])
                return qb, qc

            def rot_b(ctx, is_q, hc):
                qb, qc = ctx
                b2 = bank()
                mm(ps[b2][:, 0:nt], [(swap_bf[:, :], tmb[qb][:, 0:nt])], reads=["swap_bf", ("tmb", qb)], writes=[("ps", b2)])
                qs = tslot()
                S.op("dve", lambda e: e.tensor_tensor(out=tmp[qs][:, 0:nt], in0=ps[b2][:, 0:nt], in1=cs_t[:, 1, 0:nt], op=ALU.mult),
                     reads=[("ps", b2), "cs_t"], writes=[("tmp", qs)])
                if is_q:
                    S.op("pool", lambda e: e.tensor_tensor(out=BB[:, hc, 0:nt], in0=tmp[qc][:, 0:nt], in1=tmp[qs][:, 0:nt], op=ALU.add),
                         reads=[("tmp", qc), ("tmp", qs)], writes=[("BB", hc)])

                    def fn(e):
                        ins = None
                        for tc in range(nch):
                            ins = e.tensor_tensor(out=B2[:, hc, tc * 128:(tc + 1) * 128], in0=BB[:, hc, tc * 128:(tc + 1) * 128],
                                                  in1=qd[:, hc, :], op=ALU.mult)
                        return ins
                    S.op("pool", fn, reads=[("BB", hc), "qd"], writes=[("B2", hc)])
                else:
                    S.op("dve", lambda e: e.tensor_tensor(out=BB[:, 8 + hc, 0:nt], in0=tmp[qc][:, 0:nt], in1=tmp[qs][:, 0:nt], op=ALU.add),
                         reads=[("tmp", qc), ("tmp", qs)], writes=[("BB", 8 + hc)])

            items = []
            for hc in range(8):
                if not is_meta:
                    items.append((hc, True, hc))
                items.append((8 + hc, False, hc))
            prev = None
            for (m_glob, is_q, hc) in items:
                ctx = rot_a(m_glob)
                if prev is not None:
                    rot_b(*prev)
                prev = (ctx, is_q, hc)
                yield
            rot_b(*prev)
            yield

            kcol = 8 if is_meta else 0
            for tc in range(nch):
                for half in range(2):
                    b = bank()

                    def fn(e, tc=tc, half=half, b=b):
                        ins = None
                        for j in range(4):
                            hc = half * 4 + j
                            ins = e.matmul(ps[b][0:cs, j * 128:(j + 1) * 128], BB[:, 8 + hc, tc * 128:tc * 128 + cs],
                                           ident_bf[:, :], start=True, stop=True)
                        return ins
                    S.op("pe", fn, reads=[("BB", 8 + half * 4 + j) for j in range(4)] + ["ident_bf"], writes=[("ps", b)])

                    def fn2(e, tc=tc, half=half, b=b):
                        ins = None
                        for j in range(4):
                            hc = half * 4 + j
                            ins = e.activation(out=k_dec[0:cs, tc, hc, :], in_=ps[b][0:cs, j * 128:(j + 1) * 128], func=AF.Identity,
                                               scale=kd[0:cs, kcol + hc:kcol + hc + 1])
                        return ins
                    S.op("act", fn2, reads=[("ps", b), "kd"], writes=[("k_dec", tc, half)])
                yield

            for m in range(8):
                s = load_w("in", 16 + m, 0, 8)
                b = bank()

                def fn(e, s=s, b=b):
                    ins = None
                    for tc in range(nch):
                        for kc in range(8):
                            ins = e.matmul(ps[b][0:cs, tc * 128:(tc + 1) * 128], u[:, kc, tc * 128:tc * 128 + cs], wsl[s][:, kc, :],
                                           start=(kc == 0), stop=(kc == 7))
                    return ins
                S.op("pe", fn, reads=[("w", s)] + u_all, writes=[("ps", b)])
                src = ps[b][0:cs, 0:nch * 128].rearrange("p (a c) -> p a c", a=nch)
                dst = v_tm[0:cs, 0:nch, m * 128:(m + 1) * 128]
                if m % 2 == 0:
                    S.op("act", lambda e, src=src, dst=dst: e.activation(out=dst, in_=src, func=AF.Copy),
                         reads=[("ps", b)], writes=[("v_tm", m)])
                else:
                    S.op("dve", lambda e, src=src, dst=dst: e.tensor_copy(out=dst, in_=src),
                         reads=[("ps", b)], writes=[("v_tm", m)])
                yield

            if is_meta:
                for half in range(2):
                    b = bank()

                    def fn(e, half=half, b=b):
                        ins = None
                        for j in range(4):
                            hc = half * 4 + j
                            ins = e.matmul(ps[b][:, j * 128:(j + 1) * 128], k_dec[0:cs, 0, hc, :], v_tm[0:cs, 0, hc * 128:(hc + 1) * 128],
                                           start=True, stop=True)
                        return ins
                    S.op("pe", fn, reads=[("k_dec", 0, half)] + [("v_tm", half * 4 + j) for j in range(4)], writes=[("ps", b)])
                    S.op("dve", lambda e, half=half, b=b: e.tensor_copy(
                        out=state[:, half * 4:(half + 1) * 4, :], in_=ps[b][:, :].rearrange("p (a c) -> p a c", a=4)),
                        reads=[("ps", b)], writes=[("state", half)])
                    S.op("act", lambda e, half=half: e.activation(
                        out=state_bf[0][:, half * 4:(half + 1) * 4, :], in_=state[:, half * 4:(half + 1) * 4, :], func=AF.Copy),
                        reads=[("state", half)], writes=[("state_bf", 0, half)])
            else:
                for m in range(8):
                    b = proj_fm("in", 24 + m, u_rhs, u_all, 8, nt)
                    S.op("act", lambda e, m=m, b=b: e.activation(out=BB[:, 16 + m, 0:nt], in_=ps[b][:, 0:nt], func=AF.Silu),
                         reads=[("ps", b)], writes=[("BB", 16 + m)])
                    yield

            def ret_gen():
                def stage_A(tc):
                    tsl = slice(tc * 128, (tc + 1) * 128)
                    sti = tc % 2
                    for half in range(2):
                        b = bank()

                        def fn(e, half=half, b=b):
                            ins = None
                            for j in range(4):
                                hc = half * 4 + j
                                ins = e.matmul(ps[b][:, j * 128:(j + 1) * 128], BB[:, 8 + hc, tsl], BB[:, hc, tsl], start=True, stop=True)
                            return ins
                        S.op("pe", fn, reads=[("BB", 8 + half * 4 + j) for j in range(4)] + [("BB", half * 4 + j) for j in range(4)],
                             writes=[("ps", b)])
                        S.op("dve", lambda e, half=half, b=b: e.tensor_tensor(
                            out=stT[sti][:, half * 4:(half + 1) * 4, :], in0=ps[b][:, :].rearrange("p (a c) -> p a c", a=4),
                            in1=maskT[:, half * 4:(half + 1) * 4, :], op=ALU.mult),
                            reads=[("ps", b), "maskT"], writes=[("stT", sti, half)])

                def stage_U(tc):
                    nb = (tc + 1) % 2
                    for half in range(2):
                        bk = bank()

                        def fn(e, half=half, bk=bk):
                            ins = None
                            for j in range(4):
                                hc = half * 4 + j
                                ins = e.matmul(ps[bk][:, j * 128:(j + 1) * 128], k_dec[:, tc, hc, :], v_tm[:, tc, hc * 128:(hc + 1) * 128],
                                               start=True, stop=True)
                            return ins
                        S.op("pe", fn, reads=[("k_dec", tc, half)] + [("v_tm", half * 4 + j) for j in range(4)], writes=[("ps", bk)])

                        def fn2(e, half=half, bk=bk):
                            ins = None
                            for j in range(4):
                                hc = half * 4 + j
                                ins = e.scalar_tensor_tensor(out=state[:, hc, :], in0=state[:, hc, :], scalar=float(CD[hc]),
                                                             in1=ps[bk][:, j * 128:(j + 1) * 128], op0=ALU.mult, op1=ALU.add)
                            return ins
                        S.op("dve", fn2, reads=[("ps", bk), ("state", half)], writes=[("state", half)])
                        S.op("act", lambda e, half=half: e.activation(out=state_bf[nb][:, half * 4:(half + 1) * 4, :],
                                                                      in_=state[:, half * 4:(half + 1) * 4, :], func=AF.Copy),
                             reads=[("state", half)], writes=[("state_bf", nb, half)])

                def stage_B(tc, half):
                    tsl = slice(tc * 128, (tc + 1) * 128)
                    sti = tc % 2
                    bo = bank()

                    def fn(e):
                        ins = None
                        for j in range(4):
                            hc = half * 4 + j
                            e.matmul(ps[bo][:, j * 128:(j + 1) * 128], v_tm[:, tc, hc * 128:(hc + 1) * 128], stT[sti][:, hc, :],
                                     start=True, stop=False)
                            ins = e.matmul(ps[bo][:, j * 128:(j + 1) * 128], state_bf[sti][:, hc, :], B2[:, hc, tsl],
                                           start=False, stop=True)
                        return ins
                    S.op("pe", fn, reads=[("v_tm", half * 4 + j) for j in range(4)] + [("stT", sti, half), ("state_bf", sti, half)]
                         + [("B2", half * 4 + j) for j in range(4)], writes=[("ps", bo)])
                    sq = bslot()
                    S.op("act", lambda e: e.activation(out=tmb[sq][:, :], in_=ps[bo][:, :], func=AF.Square),
                         reads=[("ps", bo)], writes=[("tmb", sq)])
                    return bo, sq

                def stage_C(tc, half, ctx):
                    tsl = slice(tc * 128, (tc + 1) * 128)
                    bo, sq = ctx
                    bs = bank()
                    mm(ps[bs][:, :], [(ones_bf[:, :], tmb[sq][:, :])], reads=["ones_bf", ("tmb", sq)], writes=[("ps", bs)])
                    t1 = tslot()
                    S.op("act", lambda e: e.activation(out=tmp[t1][:, :], in_=ps[bs][:, :], func=AF.Ln, scale=1.0 / 128, bias=EPS),
                         reads=[("ps", bs)], writes=[("tmp", t1)])
                    S.op("act", lambda e: e.activation(out=tmp[t1][:, :], in_=tmp[t1][:, :], func=AF.Exp, scale=-0.5),
                         reads=[("tmp", t1)], writes=[("tmp", t1)])
                    S.op("pool", lambda e: e.tensor_tensor(
                        out=tmp[t1][:, :].rearrange("p (a c) -> p a c", a=4), in0=tmp[t1][:, :].rearrange("p (a c) -> p a c", a=4),
                        in1=BB[:, 16 + half * 4:16 + (half + 1) * 4, tsl], op=ALU.mult),
                        reads=[("tmp", t1)] + [("BB", 16 + half * 4 + j) for j in range(4)], writes=[("tmp", t1)])
                    S.op("dve", lambda e: e.tensor_tensor(
                        out=yg[:, half * 4:(half + 1) * 4, tsl], in0=ps[bo][:, :].rearrange("p (a c) -> p a c", a=4),
                        in1=tmp[t1][:, :].rearrange("p (a c) -> p a c", a=4), op=ALU.mult),
                        reads=[("ps", bo), ("tmp", t1)], writes=[("yg", half * 4 + j) for j in range(4)])

                stage_A(0)
                yield
                for tc in range(nch):
                    stage_U(tc)
                    yield
                    if tc + 1 < nch:
                        stage_A(tc + 1)
                        yield
                    c0 = stage_B(tc, 0)
                    yield
                    c1 = stage_B(tc, 1)
                    yield
                    stage_C(tc, 0, c0)
                    yield
                    stage_C(tc, 1, c1)
                    yield

            def lru_gen():
                for g in range(4):
                    for j in range(2):
                        m = 2 * g + j
                        b = proj_fm("in", 32 + m, u_rhs, u_all, 8, nt)
                        S.op("pool", lambda e, j=j, m=m: e.tensor_copy(out=lin[:, j, 0:3], in_=halo[:, m, :]),
                             reads=["halo"], writes=[("lin", j)])
                        S.op("act", lambda e, j=j, b=b: e.activation(out=lin[:, j, 3:3 + nt], in_=ps[b][:, 0:nt], func=AF.Copy),
                             reads=[("ps", b)], writes=[("lin", j)])
                        S.op("act", lambda e, j=j, m=m, b=b: e.activation(out=cc[:, j, 0:nt], in_=ps[b][:, 0:nt], func=AF.Identity,
                                                                          scale=pv[:, 24 + 3 * 8 + m:24 + 3 * 8 + m + 1],
                                                                          bias=pv[:, 56 + m:56 + m + 1]),
                             reads=[("ps", b), "pv"], writes=[("cc", j)])
                        for tap in range(3):
                            S.op("dve", lambda e, j=j, m=m, tap=tap: e.scalar_tensor_tensor(
                                out=cc[:, j, 0:nt], in0=lin[:, j, tap:tap + nt], scalar=pv[:, 24 + tap * 8 + m:24 + tap * 8 + m + 1],
                                in1=cc[:, j, 0:nt], op0=ALU.mult, op1=ALU.add),
                                reads=[("lin", j), ("cc", j), "pv"], writes=[("cc", j)])
                        S.op("pool", lambda e, j=j, m=m: e.tensor_copy(out=halo[:, m, :], in_=lin[:, j, nt:nt + 3]),
                             reads=[("lin", j)], writes=["halo"])
                        S.op("act", lambda e, j=j: e.activation(out=cbf[:, j, 0:nt], in_=cc[:, j, 0:nt], func=AF.Copy),
                             reads=[("cc", j)], writes=[("cbf", j)])
                        yield
                    c_rhs = lambda kc: cbf[:, kc, 0:nt]
                    c_res = [("cbf", 0), ("cbf", 1)]
                    for j in range(2):
                        m = 2 * g + j
                        br_ = proj_fm("a", m, c_rhs, c_res, 2, nt)
                        thr = tslot()
                        S.op("act", lambda e, thr=thr, br_=br_, m=m: e.activation(out=tmp[thr][:, 0:nt], in_=ps[br_][:, 0:nt], func=AF.Tanh,
                                                                                  scale=0.5, bias=drv[:, m:m + 1]),
                             reads=[("ps", br_), "drv"], writes=[("tmp", thr)])
                        bi_ = proj_fm("x", m, c_rhs, c_res, 2, nt)
                        thi = tslot()
                        S.op("act", lambda e, thi=thi, bi_=bi_, m=m: e.activation(out=tmp[thi][:, 0:nt], in_=ps[bi_][:, 0:nt], func=AF.Tanh,
                                                                                  scale=0.5, bias=drv[:, 8 + m:8 + m + 1]),
                             reads=[("ps", bi_), "drv"], writes=[("tmp", thi)])
                        S.op("act", lambda e, thr=thr, j=j, m=m: e.activation(out=la[:, j, 0:nt], in_=tmp[thr][:, 0:nt], func=AF.Exp,
                                                                              scale=drv[:, 24 + m:24 + m + 1], bias=drv[:, 24 + m:24 + m + 1]),
                             reads=[("tmp", thr), "drv"], writes=[("la", j)])
                        S.op("act", lambda e, thr=thr, j=j, m=m: e.activation(out=la2[:, j, 0:nt], in_=tmp[thr][:, 0:nt], func=AF.Exp,
                                                                              scale=drv[:, 16 + m:16 + m + 1], bias=drv[:, 16 + m:16 + m + 1]),
                             reads=[("tmp", thr), "drv"], writes=[("la2", j)])
                        S.op("dve", lambda e, thi=thi, j=j: e.scalar_tensor_tensor(
                            out=luu[:, j, 0:nt], in0=tmp[thi][:, 0:nt], scalar=1.0, in1=cc[:, j, 0:nt], op0=ALU.add, op1=ALU.mult),
                            reads=[("tmp", thi), ("cc", j)], writes=[("luu", j)])
                        yield
                    for j in range(2):
                        m = 2 * g + j
                        S.op("act", lambda e, j=j: e.activation(out=la2[:, j, 0:nt], in_=la2[:, j, 0:nt], func=AF.Sqrt, scale=-0.25, bias=0.25),
                             reads=[("la2", j)], writes=[("la2", j)])
                        S.op("dve", lambda e, j=j: e.tensor_tensor(out=luu[:, j, 0:nt], in0=luu[:, j, 0:nt], in1=la2[:, j, 0:nt], op=ALU.mult),
                             reads=[("luu", j), ("la2", j)], writes=[("luu", j)])
                        S.op("dve", lambda e, j=j, m=m: e.tensor_tensor_scan(out=la2[:, j, 0:nt], data0=la[:, j, 0:nt], data1=luu[:, j, 0:nt],
                                                                             initial=hst[:, m:m + 1], op0=ALU.mult, op1=ALU.add),
                             reads=[("la", j), ("luu", j), "hst", ("la2", j)], writes=[("la2", j)])
                        S.op("pool", lambda e, j=j, m=m: e.tensor_copy(out=hst[:, m:m + 1], in_=la2[:, j, nt - 1:nt]),
                             reads=[("la2", j)], writes=["hst"])
                        yield
                        if not is_meta:
                            b = proj_fm("in", 40 + m, u_rhs, u_all, 8, nt)
                            gl = tslot()
                            S.op("act", lambda e, gl=gl, b=b: e.activation(out=tmp[gl][:, 0:nt], in_=ps[b][:, 0:nt], func=AF.Gelu_apprx_tanh),
                                 reads=[("ps", b)], writes=[("tmp", gl)])
                            S.op("dve", lambda e, gl=gl, j=j, m=m: e.tensor_tensor(out=ylg[:, m, 0:nt], in0=tmp[gl][:, 0:nt], in1=la2[:, j, 0:nt],
                                                                                   op=ALU.mult),
                                 reads=[("tmp", gl), ("la2", j)], writes=[("ylg", m)])
                            yield

            yield from roundrobin([lru_gen(), None if is_meta else ret_gen()])
            if is_meta:
                return

            for m in range(8):
                bA = proj_fm("br", m, lambda kc: yg[:, kc, 0:nt], yg_all, 8, nt)
                bC = proj_fm("in", 48 + m, u_rhs, u_all, 8, nt)
                tha = tslot()
                S.op("act", lambda e, tha=tha, bC=bC: e.activation(out=tmp[tha][:, 0:nt], in_=ps[bC][:, 0:nt], func=AF.Tanh, scale=0.5),
                     reads=[("ps", bC)], writes=[("tmp", tha)])
                S.op("dve", lambda e, tha=tha, bA=bA: e.scalar_tensor_tensor(out=tmp[tha][:, 0:nt], in0=tmp[tha][:, 0:nt], scalar=1.0,
                                                                             in1=ps[bA][:, 0:nt], op0=ALU.add, op1=ALU.mult),
                     reads=[("tmp", tha), ("ps", bA)], writes=[("tmp", tha)])
                bB = proj_fm("bl", m, lambda kc: ylg[:, kc, 0:nt], ylg_all, 8, nt)
                bD = proj_fm("in", 56 + m, u_rhs, u_all, 8, nt)
                thb = tslot()
                S.op("act", lambda e, thb=thb, bD=bD: e.activation(out=tmp[thb][:, 0:nt], in_=ps[bD][:, 0:nt], func=AF.Tanh, scale=0.5),
                     reads=[("ps", bD)], writes=[("tmp", thb)])
                S.op("dve", lambda e, thb=thb, bB=bB: e.scalar_tensor_tensor(out=tmp[thb][:, 0:nt], in0=tmp[thb][:, 0:nt], scalar=1.0,
                                                                             in1=ps[bB][:, 0:nt], op0=ALU.add, op1=ALU.mult),
                     reads=[("tmp", thb), ("ps", bB)], writes=[("tmp", thb)])
                S.op("pool", lambda e, tha=tha, thb=thb, m=m: e.tensor_tensor(out=BB[:, 8 + m, 0:nt], in0=tmp[tha][:, 0:nt],
                                                                              in1=tmp[thb][:, 0:nt], op=ALU.add),
                     reads=[("tmp", tha), ("tmp", thb)], writes=[("BB", 8 + m)])
                yield
            mx_all = [("BB", 8 + k) for k in range(8)]
            for m in range(8):
                b = proj_fm("o", m, lambda kc: BB[:, 8 + kc, 0:nt], mx_all, 8, nt)
                S.op("dve", lambda e, m=m, b=b: e.scalar_tensor_tensor(out=hb[:, m, 0:nt], in0=ps[b][:, 0:nt], scalar=0.5, in1=hb[:, m, 0:nt],
                                                                       op0=ALU.mult, op1=ALU.add),
                     reads=[("ps", b), (hk, m)], writes=[(hk, m)])
                yield
            yield from rmsnorm(hb, hk, nt, 8, u, "u", lambda fc: u[:, fc, 0:nt], u_res)
            for jf in range(NFF):
                bG = proj_fm("fi", jf, u_rhs, u_all, 8, nt)
                sg_ = tslot()
                S.op("act", lambda e, sg_=sg_, bG=bG: e.activation(out=tmp[sg_][:, 0:nt], in_=ps[bG][:, 0:nt], func=AF.Silu),
                     reads=[("ps", bG)], writes=[("tmp", sg_)])
                bU = proj_fm("fi", NFF + jf, u_rhs, u_all, 8, nt)
                S.op("dve", lambda e, sg_=sg_, bU=bU, jf=jf: e.tensor_tensor(out=BB[:, jf, 0:nt], in0=tmp[sg_][:, 0:nt], in1=ps[bU][:, 0:nt],
                                                                             op=ALU.mult),
                     reads=[("tmp", sg_), ("ps", bU)], writes=[("BB", jf)])
                yield

        def tile_tail(ti):
            nt = NT
            nch = NT // 128
            hb = hh[ti % 2]
            hk = "h%d" % (ti % 2)
            act_all = [("BB", k) for k in range(NFF)]
            for m in range(8):
                b = proj_fm("fo", m, lambda kc: BB[:, kc, 0:nt], act_all, NFF, nt)
                S.op("dve", lambda e, m=m, b=b: e.tensor_tensor(out=hb[:, m, 0:nt], in0=ps[b][:, 0:nt], in1=hb[:, m, 0:nt], op=ALU.add),
                     reads=[("ps", b), (hk, m)], writes=[(hk, m)])
                yield
            yield from rmsnorm(hb, hk, nt, 16, yg, "yg", lambda fc: hb[:, fc, 0:nt], lambda fc: (hk, fc))
            for tc in range(nch):
                ob = tc % 2
                for half in range(2):
                    b = bank()

                    def fn(e, half=half, b=b, tc=tc):
                        ins = None
                        for j in range(4):
                            fc = half * 4 + j
                            ins = e.transpose(out=ps[b][:, j * 128:(j + 1) * 128], in_=hb[:, fc, tc * 128:(tc + 1) * 128], identity=ident[:, :])
                        return ins
                    S.op("pe", fn, reads=[(hk, half * 4 + j) for j in range(4)] + ["ident"], writes=[("ps", b)])
                    if half == 0:
                        S.op("act", lambda e, b=b, ob=ob: e.activation(out=otm[ob][:, 0:512], in_=ps[b][:, :], func=AF.Copy),
                             reads=[("ps", b)], writes=[("otm", ob, 0)])
                    else:
                        S.op("dve", lambda e, b=b, ob=ob: e.tensor_copy(out=otm[ob][:, 512:1024], in_=ps[b][:, :]),
                             reads=[("ps", b)], writes=[("otm", ob, 1)])
                r0 = ti * NT + tc * 128
                S.dma("sp", lambda e, r0=r0, ob=ob: e.dma_start(out=out[r0:r0 + 128, :], in_=otm[ob][:, :]), ("ost", ob),
                      reads=[("otm", ob, 0), ("otm", ob, 1)], writes=[("out", ob)])
                yield

        log_g = np.log(1.0 - 2.0 ** (-5.0 - np.arange(8, dtype=np.float64)))
        CD = np.exp(128.0 * log_g)

        def run_all(g):
            for _ in g:
                pass

        run_all(tile_head(-1))
        run_all(tile_head(0))
        for ti in range(n_tiles):
            run_all(roundrobin([tile_tail(ti), tile_head(ti + 1) if ti + 1 < n_tiles else None]))
        S.final_wait("sp")
        with nc.Block() as block:
            S.emit(block)
    return nc


def host_consts(T):
    inv = (np.float32(10000.0) ** (-(np.arange(0, 128, 2, dtype=np.float32)) / np.float32(128))).astype(np.float32)
    pos = np.arange(T, dtype=np.float32)
    ang = (pos[None, :] * inv[:, None]).astype(np.float32).astype(np.float64)
    cos, sin = np.cos(ang), np.sin(ang)
    cs = np.stack([np.concatenate([cos, cos], 0), np.concatenate([-sin, sin], 0)], 1).astype(np.float32)
    log_g = np.log(1.0 - 2.0 ** (-5.0 - np.arange(8, dtype=np.float64)))
    idx = np.arange(128)
    diff = idx[None, :] - idx[:, None]
    maskT = np.where(diff[None] >= 0, np.exp(np.maximum(diff, 0)[None] * log_g[:, None, None]), 0.0) * 128.0 ** -0.5
    maskT = np.ascontiguousarray(maskT.transpose(1, 0, 2)).astype(np.float32)
    qd = np.exp((idx + 1.0)[None, :] * log_g[:, None])
    qd = np.ascontiguousarray(np.broadcast_to(qd[None], (128, 8, 128))).astype(np.float32)
    kd = np.zeros((128, 16), np.float64)
    kd[:, 0:8] = np.exp((127.0 - idx)[:, None] * log_g[None, :]) * 128.0 ** -0.5
    kd[:16, 8:16] = np.exp((15.0 - np.arange(16))[:, None] * log_g[None, :]) * 128.0 ** -0.5
    ident = np.eye(128, dtype=np.float32)
    swapP = np.zeros((128, 128), np.float32)
    swapP[(np.arange(128) + 64) % 128, np.arange(128)] = 1.0
    return dict(cs_tab=cs, maskT=maskT, qd=qd, kd=kd.astype(np.float32), ident=ident, swapP=swapP)


def col_layout(v):
    return np.ascontiguousarray(np.asarray(v, np.float32).reshape(8, 128).T)


_NC_CACHE = {}


def run(inputs, n_tiles, n_cores):
    x = np.asarray(inputs["x"], np.float32)
    T = NMETA + n_tiles * NT
    consts = host_consts(T)
    cw = np.asarray(inputs["conv_w"], np.float32)[0]
    pv = np.concatenate(
        [col_layout(inputs["mix_norm_w"][0]), col_layout(inputs["ffn_norm_w"][0]), col_layout(inputs["final_norm_w"])]
        + [col_layout(cw[j]) for j in range(4)]
        + [col_layout(inputs["conv_b"][0]), col_layout(inputs["lru_ba"][0]), col_layout(inputs["lru_bx"][0]),
           col_layout(inputs["lru_lambda"][0])], axis=1)
    assert pv.shape == (128, 88)
    shared = dict(
        meta_tokens=np.asarray(inputs["meta_tokens"], np.float32),
        w_in=np.asarray(inputs["w_in"], np.float32)[0],
        w_branch_ret=np.asarray(inputs["w_branch_ret"], np.float32)[0],
        w_branch_lru=np.asarray(inputs["w_branch_lru"], np.float32)[0],
        w_out=np.asarray(inputs["w_out"], np.float32)[0],
        lru_wa=np.asarray(inputs["lru_wa"], np.float32)[0],
        lru_wx=np.asarray(inputs["lru_wx"], np.float32)[0],
        w_ffn_in=np.asarray(inputs["w_ffn_in"], np.float32)[0],
        w_ffn_out=np.asarray(inputs["w_ffn_out"], np.float32)[0],
        pv=np.ascontiguousarray(pv), **consts)
    if n_tiles not in _NC_CACHE:
        _NC_CACHE[n_tiles] = build(n_tiles)
    nc = _NC_CACHE[n_tiles]
    in_maps = [dict(shared, x=np.ascontiguousarray(x[b, :n_tiles * NT])) for b in range(n_cores)]
    res = run_bass_kernel_spmd(nc, in_maps, core_ids=list(range(n_cores)))
    return np.stack([np.asarray(res.results[b]["out"]) for b in range(n_cores)], 0).astype(np.float32)


def kernel(**inputs):
    x = inputs["x"]
    return run(inputs, x.shape[1] // NT, x.shape[0])
```

```python
import contextlib
import numpy as np
import concourse.bass as bass
import concourse.mybir as mybir
from concourse.bass_utils import run_bass_kernel_spmd

F32 = mybir.dt.float32
BF16 = mybir.dt.bfloat16
ALU = mybir.AluOpType
AF = mybir.ActivationFunctionType

D = 1024
NMETA = 16
NH = 8
FF = 2816
NFF = FF // 128
NT = 512
EPS = 1e-6
SEM_LIMIT = 30000
NWS = 8
NTMP = 10
NTB = 3
NPS = 8


class Counter:
    def __init__(self, sched, step):
        self.sched = sched
        self.step = step
        self.sems = [sched.new_sem()]
        self.val = 0

    def next_token(self):
        if self.val + self.step > SEM_LIMIT:
            self.sems.append(self.sched.new_sem())
            self.val = 0
        self.val += self.step
        return (self, len(self.sems) - 1, self.val)


class Op:
    __slots__ = ("idx", "eng", "fn", "key", "deps", "cost", "af", "users", "nd", "ready_t", "fin", "tok")

    def __init__(self, idx, eng, fn, key, cost, af):
        self.idx = idx
        self.eng = eng
        self.fn = fn
        self.key = key
        self.deps = {}
        self.cost = cost
        self.af = af
        self.users = []
        self.nd = 0
        self.ready_t = 0.0
        self.fin = 0.0
        self.tok = None


ACT_SETS = {"exp": ("T", "E"), "sqrt": ("Q",), "gelu": ("G", "T"), "silu": ("S", "T"), "ln": ("L",)}
ACT_HOME = {"T": "exp", "E": "exp", "Q": "sqrt", "G": "gelu", "S": "silu", "L": "ln"}
DEF_COST = {"pe": 0.3, "act": 0.6, "dve": 0.65, "pool": 1.3, "sp": 0.1}
SEM_LAT = 0.25
TBL_LOAD = 1.3


class Sched:
    ENGS = ("pe", "act", "dve", "pool", "sp")

    def __init__(self, nc, stack, n_sems):
        self.nc = nc
        self.free_sems = [stack.enter_context(nc.semaphore(f"s{i}")) for i in range(n_sems)]
        self.ops = []
        self.last_w = {}
        self.readers = {}

    def new_sem(self):
        return self.free_sems.pop()

    def _add(self, op, reads, writes):
        deps = op.deps

        def dep(o, wait=True):
            if o is op:
                return
            deps[o] = deps.get(o, False) or wait
        for r in reads:
            o = self.last_w.get(r)
            if o is not None:
                dep(o)
            if isinstance(r, tuple) and r[0] == "ps":
                for o in self.readers.get(r, ()):
                    if o.eng != op.eng or o.key is not None:
                        dep(o)
        for w in writes:
            o = self.last_w.get(w)
            if o is not None:
                dep(o)
            for o in self.readers.get(w, ()):
                same_eng = (o.eng == op.eng and o.key is None and op.key is None)
                dep(o, wait=not same_eng)
        for r in reads:
            self.readers.setdefault(r, []).append(op)
        for w in writes:
            self.last_w[w] = op
            self.readers[w] = []
        self.ops.append(op)

    def op(self, eng, fn, reads=(), writes=(), c=None, af=None):
        o = Op(len(self.ops), eng, fn, None, DEF_COST[eng] if c is None else c, af)
        self._add(o, reads, writes)
        return o

    def dma(self, eng, fn, key, reads=(), writes=(), c=None):
        o = Op(len(self.ops), eng, fn, key, 3.0 if c is None else c, None)
        self._add(o, reads, list(writes) + [("dmakey", key)])
        return o

    def schedule(self):
        import heapq
        ops = self.ops
        for o in ops:
            o.nd = len(o.deps)
            for d in o.deps:
                d.users.append(o)
        eng_free = {e: 0.0 for e in self.ENGS}
        act_set = [None]
        ready = {e: [] for e in self.ENGS}
        for o in ops:
            if o.nd == 0:
                heapq.heappush(ready[o.eng], (0.0, o.idx, o))
        order = {e: [] for e in self.ENGS}
        n_left = len(ops)
        LOOK = 6

        def act_pen(o):
            if o.af is None or act_set[0] is None:
                return 0.0 if o.af is None else TBL_LOAD
            return 0.0 if o.af in ACT_SETS[act_set[0]] else TBL_LOAD
        while n_left:
            best = None
            for e in self.ENGS:
                h = ready[e]
                if not h:
                    continue
                cands = heapq.nsmallest(LOOK, h) if e == "act" else [h[0]]
                for (rt, idx, o) in cands:
                    st = max(eng_free[e], rt)
                    if e == "act":
                        st += act_pen(o)
                    k = (st, idx)
                    if best is None or k < best[0]:
                        best = (k, o)
            (st, _), o = best
            e = o.eng
            h = ready[e]
            h.remove((o.ready_t, o.idx, o))
            heapq.heapify(h)
            if e == "act" and o.af is not None:
                if act_set[0] is None or o.af not in ACT_SETS[act_set[0]]:
                    act_set[0] = ACT_HOME[o.af]
            if o.key is not None:
                issue = 1.0 if e == "pool" else 0.08
                eng_free[e] = st + issue
                o.fin = st + issue + o.cost
            else:
                eng_free[e] = st + o.cost
                o.fin = st + o.cost
            order[e].append(o)
            n_left -= 1
            for u_ in o.users:
                u_.ready_t = max(u_.ready_t, o.fin + SEM_LAT)
                u_.nd -= 1
                if u_.nd == 0:
                    heapq.heappush(ready[u_.eng], (u_.ready_t, u_.idx, u_))
        self.est_us = max(o.fin for o in ops)
        return order

    def emit(self, block):
        order = self.schedule()
        ectr = {e: Counter(self, 1) for e in self.ENGS}
        dctr = {}
        for e in self.ENGS:
            for o in order[e]:
                if o.key is None:
                    o.tok = ectr[e].next_token()
                else:
                    c = dctr.get(o.key)
                    if c is None:
                        c = dctr[o.key] = Counter(self, 16)
                    o.tok = c.next_token()
        attr = {"pe": "tensor", "act": "scalar", "dve": "vector", "pool": "gpsimd", "sp": "sync"}
        final = [(c.sems[-1], c.val) for c in list(ectr.values()) + list(dctr.values()) if c.val > 0]
        for e in self.ENGS:
            stream = []
            known = {}
            for o in order[e]:
                best = {}
                for d, need in o.deps.items():
                    if not need:
                        continue
                    c, ep, v = d.tok
                    cur = best.get(c)
                    if cur is None or (ep, v) > cur:
                        best[c] = (ep, v)
                waits = []
                for c, (ep, v) in best.items():
                    cur = known.get(c)
                    if cur is not None and cur >= (ep, v):
                        continue
                    known[c] = (ep, v)
                    waits.append((c.sems[ep], v))
                stream.append((waits, o.fn, (o.tok[0].sems[o.tok[1]], 16 if o.key is not None else 1)))
            if e == "sp":
                stream.append((final, None, None))

            def body(engine, stream=stream):
                for waits, fn, inc in stream:
                    for (sem, v) in waits:
                        engine.wait_ge(sem, v)
                    if fn is not None:
                        ins = fn(engine)
                        ins.then_inc(inc[0], inc[1])
            getattr(block, attr[e])(body)


def build(n_tiles):
    nc = bass.Bass("TRN2", target_bir_lowering=False)
    T = NMETA + n_tiles * NT

    def din(name, shape):
        return nc.dram_tensor(name, shape, F32, kind="ExternalInput").ap()

    x = din("x", [n_tiles * NT, D])
    meta = din("meta_tokens", [NMETA, D])
    w_in = din("w_in", [D, 8 * D])
    w_br = din("w_branch_ret", [D, D])
    w_bl = din("w_branch_lru", [D, D])
    w_o = din("w_out", [D, D])
    w_a = din("lru_wa", [4, 256, 256])
    w_x = din("lru_wx", [4, 256, 256])
    w_fi = din("w_ffn_in", [D, 2 * FF])
    w_fo = din("w_ffn_out", [FF, D])
    pv_d = din("pv", [128, 88])
    cs_d = din("cs_tab", [128, 2, T])
    mask_d = din("maskT", [128, 8, 128])
    qd_d = din("qd", [128, 8, 128])
    kd_d = din("kd", [128, 16])
    ident_d = din("ident", [128, 128])
    swap_d = din("swapP", [128, 128])
    out = nc.dram_tensor("out", [n_tiles * NT, D], F32, kind="ExternalOutput").ap()

    def dscr(name, shape):
        return nc.dram_tensor(name, shape, BF16, kind="Internal").ap()

    ws = {
        "in": dscr("ws_in", [64, 128, 8, 128]),
        "br": dscr("ws_br", [8, 128, 8, 128]),
        "bl": dscr("ws_bl", [8, 128, 8, 128]),
        "o": dscr("ws_o", [8, 128, 8, 128]),
        "a": dscr("ws_a", [8, 128, 2, 128]),
        "x": dscr("ws_x", [8, 128, 2, 128]),
        "fi": dscr("ws_fi", [44, 128, 8, 128]),
        "fo": dscr("ws_fo", [8, 128, NFF, 128]),
    }

    with contextlib.ExitStack() as st:
        S = Sched(nc, st, n_sems=96)

        def sb(name, shape, dt=F32):
            return st.enter_context(nc.sbuf_tensor(name, shape, dt))

        pv = sb("pv_s", [128, 96])
        ident = sb("ident_s", [128, 128])
        ident_bf = sb("ident_bf", [128, 128], BF16)
        ones_bf = sb("ones_bf", [128, 128], BF16)
        swap_bf = sb("swap_bf", [128, 128], BF16)
        maskT = sb("maskT_s", [128, 8, 128])
        qd = sb("qd_s", [128, 8, 128])
        kd = sb("kd_s", [128, 16])
        drv = sb("drv", [128, 48])
        state = sb("state", [128, 8, 128])
        state_bf = [sb(f"state_bf{i}", [128, 8, 128], BF16) for i in range(2)]
        hst = sb("hst", [128, 8])
        halo = sb("halo", [128, 8, 3])
        xin = [sb(f"xin{i}", [128, D]) for i in range(2)]
        otm = [sb(f"otm{i}", [128, D]) for i in range(2)]
        hh = [sb(f"h{i}", [128, 8, NT]) for i in range(2)]
        u = sb("u", [128, 8, NT], BF16)
        cs_t = sb("cs_t", [128, 2, NT])
        BB = sb("BB", [128, 24, NT], BF16)
        B2 = sb("B2", [128, 8, NT], BF16)
        k_dec = sb("k_dec", [128, 4, 8, 128], BF16)
        v_tm = sb("v_tm", [128, 4, D], BF16)
        yg = sb("yg", [128, 8, NT], BF16)
        ylg = sb("ylg", [128, 8, NT], BF16)
        lin = sb("lin", [128, 2, NT + 3])
        cc = sb("cc", [128, 2, NT])
        cbf = sb("cbf", [128, 2, NT], BF16)
        la = sb("la", [128, 2, NT])
        la2 = sb("la2", [128, 2, NT])
        luu = sb("luu", [128, 2, NT])
        stT = [sb(f"stT{i}", [128, 8, 128], BF16) for i in range(2)]
        tmp = [sb(f"tmp{i}", [128, NT]) for i in range(NTMP)]
        tmb = [sb(f"tmb{i}", [128, NT], BF16) for i in range(NTB)]
        wsl = [sb(f"wsl{i}", [128, 8, 128], BF16) for i in range(NWS)]
        ps = [st.enter_context(nc.psum_tensor(f"ps{i}", [128, 512], F32)) for i in range(NPS)]

        rr = {"ps": 0, "tmp": 0, "tmb": 0, "w": 0, "st": 0, "cv": 0}

        def bank():
            b = rr["ps"]
            rr["ps"] = (b + 1) % NPS
            return b

        def tslot():
            b = rr["tmp"]
            rr["tmp"] = (b + 1) % NTMP
            return b

        def bslot():
            b = rr["tmb"]
            rr["tmb"] = (b + 1) % NTB
            return b

        def ld(dst, src, key):
            S.dma("sp", lambda e: e.dma_start(out=dst, in_=src), key, writes=[key])

        ld(pv[:, 0:88], pv_d, "pv")
        ld(ident[:], ident_d, "ident")
        ld(maskT[:], mask_d, "maskT")
        ld(qd[:], qd_d, "qd")
        ld(kd[:], kd_d, "kd")
        S.dma("pool", lambda e: e.dma_start(out=ident_bf[:], in_=ident_d), "ident_bf", writes=["ident_bf"])
        S.dma("pool", lambda e: e.dma_start(out=swap_bf[:], in_=swap_d), "swap_bf", writes=["swap_bf"])
        S.op("pool", lambda e: e.memset(ones_bf[:], 1.0), writes=["ones_bf"])
        S.op("pool", lambda e: e.memset(halo[:], 0.0), writes=["halo"])
        S.op("pool", lambda e: e.memset(hst[:], 0.0), writes=["hst"])
        S.op("dve", lambda e: e.tensor_scalar(out=drv[:, 0:16], in0=pv[:, 64:80], scalar1=0.5, scalar2=None, op0=ALU.mult),
             reads=["pv"], writes=["drv"])
        S.op("act", lambda e: e.activation(out=drv[:, 32:40], in_=pv[:, 80:88], func=AF.Exp, scale=-1.0),
             reads=["pv"], writes=["drv_t"], af="E")
        S.op("act", lambda e: e.activation(out=drv[:, 40:48], in_=drv[:, 32:40], func=AF.Ln, bias=1.0),
             reads=["drv_t"], writes=["drv_t2"], af="L")
        S.op("dve", lambda e: e.tensor_scalar(out=drv[:, 16:24], in0=drv[:, 40:48], scalar1=-8.0, scalar2=None, op0=ALU.mult),
             reads=["drv_t2", "drv"], writes=["drv"])
        S.op("dve", lambda e: e.tensor_scalar(out=drv[:, 24:32], in0=drv[:, 40:48], scalar1=-4.0, scalar2=None, op0=ALU.mult),
             reads=["drv_t2", "drv"], writes=["drv"])

        def conv_w(key, n_m, src_fn):
            for m in range(n_m):
                i = rr["cv"]
                rr["cv"] = (i + 1) % 4
                S.dma("pool", lambda e, m=m: e.dma_start(out=ws[key][m], in_=src_fn(m)), ("wsd", i),
                      writes=[("ws", key, m)], c=8.0)

        def cols(wd):
            return lambda m: wd[:, m * 128:(m + 1) * 128].rearrange("(kc p) c -> p kc c", p=128)

        def gate_src(wd):
            return lambda m: wd[m // 2][:, (m % 2) * 128:(m % 2 + 1) * 128].rearrange("(kc p) c -> p kc c", p=128)

        conv_w("in", 64, cols(w_in))
        conv_w("a", 8, gate_src(w_a))
        conv_w("x", 8, gate_src(w_x))
        conv_w("br", 8, cols(w_br))
        conv_w("bl", 8, cols(w_bl))
        conv_w("o", 8, cols(w_o))
        conv_w("fi", 44, cols(w_fi))
        conv_w("fo", 8, cols(w_fo))

        def load_w(key, m, kc0, n):
            s = rr["w"]
            rr["w"] = (s + 1) % NWS
            S.dma("sp", lambda e: e.dma_start(out=wsl[s][:, 0:n, :], in_=ws[key][m][:, kc0:kc0 + n, :]), ("wl", s),
                  reads=[("ws", key, m)], writes=[("w", s)])
            return s

        def mm(out_ap, pairs, reads, writes):
            def fn(e):
                n = len(pairs)
                ins = None
                for i, (l, r) in enumerate(pairs):
                    ins = e.matmul(out_ap, l, r, start=(i == 0), stop=(i == n - 1))
                return ins
            nfree = out_ap.shape[-1]
            S.op("pe", fn, reads=reads, writes=writes, c=len(pairs) * (max(nfree, 128) / 2400.0 + 0.01))

        def proj_fm(key, m, rhs_fn, rhs_res, nkc, nt):
            slots = []
            kc0 = 0
            while kc0 < nkc:
                n = min(8, nkc - kc0)
                slots.append((load_w(key, m, kc0, n), kc0, n))
                kc0 += n
            b = bank()
            pairs = []
            for (s, k0, n) in slots:
                for j in range(n):
                    pairs.append((wsl[s][:, j, :], rhs_fn(k0 + j)))
            mm(ps[b][:, 0:nt], pairs, reads=[("w", s) for (s, _, _) in slots] + rhs_res, writes=[("ps", b)])
            return b

        u_res = lambda fc: ("u", fc)
        u_all = [("u", fc) for fc in range(8)]
        yg_all = [("yg", k) for k in range(8)]
        ylg_all = [("ylg", k) for k in range(8)]

        def rmsnorm(hb, hk, nt, wcol, scr, scr_key, dst_fn, dst_res_fn):
            for fc in range(8):
                S.op("act", lambda e, fc=fc: e.activation(out=scr[:, fc, 0:nt], in_=hb[:, fc, 0:nt], func=AF.Square),
                     reads=[(hk, fc)], writes=[(scr_key, fc)])
            yield
            b = bank()
            mm(ps[b][:, 0:nt], [(ones_bf[:, :], scr[:, fc, 0:nt]) for fc in range(8)],
               reads=["ones_bf"] + [(scr_key, fc) for fc in range(8)], writes=[("ps", b)])
            t1 = tslot()
            S.op("act", lambda e: e.activation(out=tmp[t1][:, 0:nt], in_=ps[b][:, 0:nt], func=AF.Ln, scale=1.0 / D, bias=EPS),
                 reads=[("ps", b)], writes=[("tmp", t1)], af="L")
            S.op("act", lambda e: e.activation(out=tmp[t1][:, 0:nt], in_=tmp[t1][:, 0:nt], func=AF.Exp, scale=-0.5),
                 reads=[("tmp", t1)], writes=[("tmp", t1)], af="E")
            yield
            for fc in range(8):
                S.op("dve", lambda e, fc=fc: e.scalar_tensor_tensor(
                    out=dst_fn(fc), in0=hb[:, fc, 0:nt], scalar=pv[:, wcol + fc:wcol + fc + 1], in1=tmp[t1][:, 0:nt],
                    op0=ALU.mult, op1=ALU.mult),
                    reads=[(hk, fc), ("tmp", t1), "pv"], writes=[dst_res_fn(fc)])
                if fc % 4 == 3:
                    yield

        def roundrobin(gens):
            gens = [g for g in gens if g is not None]
            while gens:
                for g in list(gens):
                    try:
                        next(g)
                        yield
                    except StopIteration:
                        gens.remove(g)

        def tile_head(ti):
            is_meta = ti < 0
            nt = NMETA if is_meta else NT
            nch = 1 if is_meta else NT // 128
            cs = NMETA if is_meta else 128
            pos0 = 0 if is_meta else NMETA + ti * NT
            hb = hh[ti % 2]
            hk = "h%d" % (ti % 2)
            u_rhs = lambda kc: u[:, kc, 0:nt]

            S.dma("sp", lambda e: e.dma_start(out=cs_t[:, :, 0:nt], in_=cs_d[:, :, pos0:pos0 + nt]), "cs_t",
                  writes=["cs_t"])
            for tc in range(nch):
                xb = tc % 2
                if is_meta:
                    S.dma("sp", lambda e: e.dma_start(out=xin[0][0:NMETA, :], in_=meta), ("xin", 0), writes=[("xin", 0)])
                else:
                    r0 = ti * NT + tc * 128
                    S.dma("sp", lambda e, r0=r0, xb=xb: e.dma_start(out=xin[xb][:, :], in_=x[r0:r0 + 128, :]), ("xin", xb),
                          writes=[("xin", xb)])
                for half in range(2):
                    b = bank()

                    def fn(e, half=half, b=b, xb=xb):
                        ins = None
                        for j in range(4):
                            fc = half * 4 + j
                            ins = e.transpose(out=ps[b][:, j * 128:j * 128 + cs], in_=xin[xb][0:cs, fc * 128:(fc + 1) * 128],
                                              identity=ident[0:cs, 0:cs])
                        return ins
                    S.op("pe", fn, reads=[("xin", xb), "ident"], writes=[("ps", b)])
                    src = ps[b][:, :].rearrange("p (a c) -> p a c", a=4)[:, :, 0:cs]
                    dst = hb[:, half * 4:(half + 1) * 4, tc * 128:tc * 128 + cs]
                    if half == 0:
                        S.op("act", lambda e, src=src, dst=dst: e.activation(out=dst, in_=src, func=AF.Copy),
                             reads=[("ps", b)], writes=[(hk, half * 4 + j) for j in range(4)])
                    else:
                        S.op("dve", lambda e, src=src, dst=dst: e.tensor_copy(out=dst, in_=src),
                             reads=[("ps", b)], writes=[(hk, half * 4 + j) for j in range(4)])
                yield

            yield from rmsnorm(hb, hk, nt, 0, u, "u", lambda fc: u[:, fc, 0:nt], u_res)

            def rot_a(m_glob):
                b1 = proj_fm("in", m_glob, u_rhs, u_all, 8, nt)
                qb = bslot()
                S.op("act", lambda e: e.activation(out=tmb[qb][:, 0:nt], in_=ps[b1][:, 0:nt], func=AF.Copy),
                     reads=[("ps", b1)], writes=[("tmb", qb)])
                qc = tslot()
                S.op("dve", lambda e: e.tensor_tensor(out=tmp[qc][:, 0:nt], in0=tmb[qb][:, 0:nt], in1=cs_t[:, 0, 0:nt], op=ALU.mult),
                     reads=[("tmb", qb), "cs_t"], writes=[("tmp", qc)])
                return qb, qc

            def rot_b(ctx, is_q, hc):
                qb, qc = ctx
                b2 = bank()
                mm(ps[b2][:, 0:nt], [(swap_bf[:, :], tmb[qb][:, 0:nt])], reads=["swap_bf", ("tmb", qb)], writes=[("ps", b2)])
                qs = tslot()
                S.op("dve", lambda e: e.tensor_tensor(out=tmp[qs][:, 0:nt], in0=ps[b2][:, 0:nt], in1=cs_t[:, 1, 0:nt], op=ALU.mult),
                     reads=[("ps", b2), "cs_t"], writes=[("tmp", qs)])
                if is_q:
                    S.op("pool", lambda e: e.tensor_tensor(out=BB[:, hc, 0:nt], in0=tmp[qc][:, 0:nt], in1=tmp[qs][:, 0:nt], op=ALU.add),
                         reads=[("tmp", qc), ("tmp", qs)], writes=[("BB", hc)])

                    def fn(e):
                        ins = None
                        for tc in range(nch):
                            ins = e.tensor_tensor(out=B2[:, hc, tc * 128:(tc + 1) * 128], in0=BB[:, hc, tc * 128:(tc + 1) * 128],
                                                  in1=qd[:, hc, :], op=ALU.mult)
                        return ins
                    S.op("pool", fn, reads=[("BB", hc), "qd"], writes=[("B2", hc)], c=1.7)
                else:
                    S.op("dve", lambda e: e.tensor_tensor(out=BB[:, 8 + hc, 0:nt], in0=tmp[qc][:, 0:nt], in1=tmp[qs][:, 0:nt], op=ALU.add),
                         reads=[("tmp", qc), ("tmp", qs)], writes=[("BB", 8 + hc)])

            items = []
            for hc in range(8):
                if not is_meta:
                    items.append((hc, True, hc))
                items.append((8 + hc, False, hc))
            prev = None
            for (m_glob, is_q, hc) in items:
                ctx = rot_a(m_glob)
                if prev is not None:
                    rot_b(*prev)
                prev = (ctx, is_q, hc)
                yield
            rot_b(*prev)
            yield

            kcol = 8 if is_meta else 0
            for tc in range(nch):
                for half in range(2):
                    b = bank()

                    def fn(e, tc=tc, half=half, b=b):
                        ins = None
                        for j in range(4):
                            hc = half * 4 + j
                            ins = e.matmul(ps[b][0:cs, j * 128:(j + 1) * 128], BB[:, 8 + hc, tc * 128:tc * 128 + cs],
                                           ident_bf[:, :], start=True, stop=True)
                        return ins
                    S.op("pe", fn, reads=[("BB", 8 + half * 4 + j) for j in range(4)] + ["ident_bf"], writes=[("ps", b)])

                    def fn2(e, tc=tc, half=half, b=b):
                        ins = None
                        for j in range(4):
                            hc = half * 4 + j
                            ins = e.activation(out=k_dec[0:cs, tc, hc, :], in_=ps[b][0:cs, j * 128:(j + 1) * 128], func=AF.Identity,
                                               scale=kd[0:cs, kcol + hc:kcol + hc + 1])
                        return ins
                    S.op("act", fn2, reads=[("ps", b), "kd"], writes=[("k_dec", tc, half)])
                yield

            for m in range(8):
                s = load_w("in", 16 + m, 0, 8)
                b = bank()

                def fn(e, s=s, b=b):
                    ins = None
                    for tc in range(nch):
                        for kc in range(8):
                            ins = e.matmul(ps[b][0:cs, tc * 128:(tc + 1) * 128], u[:, kc, tc * 128:tc * 128 + cs], wsl[s][:, kc, :],
                                           start=(kc == 0), stop=(kc == 7))
                    return ins
                S.op("pe", fn, reads=[("w", s)] + u_all, writes=[("ps", b)], c=nch * 8 * 0.065)
                src = ps[b][0:cs, 0:nch * 128].rearrange("p (a c) -> p a c", a=nch)
                dst = v_tm[0:cs, 0:nch, m * 128:(m + 1) * 128]
                if m % 2 == 0:
                    S.op("act", lambda e, src=src, dst=dst: e.activation(out=dst, in_=src, func=AF.Copy),
                         reads=[("ps", b)], writes=[("v_tm", m)])
                else:
                    S.op("dve", lambda e, src=src, dst=dst: e.tensor_copy(out=dst, in_=src),
                         reads=[("ps", b)], writes=[("v_tm", m)])
                yield

            if is_meta:
                for half in range(2):
                    b = bank()

                    def fn(e, half=half, b=b):
                        ins = None
                        for j in range(4):
                            hc = half * 4 + j
                            ins = e.matmul(ps[b][:, j * 128:(j + 1) * 128], k_dec[0:cs, 0, hc, :], v_tm[0:cs, 0, hc * 128:(hc + 1) * 128],
                                           start=True, stop=True)
                        return ins
                    S.op("pe", fn, reads=[("k_dec", 0, half)] + [("v_tm", half * 4 + j) for j in range(4)], writes=[("ps", b)])
                    S.op("dve", lambda e, half=half, b=b: e.tensor_copy(
                        out=state[:, half * 4:(half + 1) * 4, :], in_=ps[b][:, :].rearrange("p (a c) -> p a c", a=4)),
                        reads=[("ps", b)], writes=[("state", half)])
                    S.op("act", lambda e, half=half: e.activation(
                        out=state_bf[0][:, half * 4:(half + 1) * 4, :], in_=state[:, half * 4:(half + 1) * 4, :], func=AF.Copy),
                        reads=[("state", half)], writes=[("state_bf", 0, half)])
            else:
                for m in range(8):
                    b = proj_fm("in", 24 + m, u_rhs, u_all, 8, nt)
                    S.op("act", lambda e, m=m, b=b: e.activation(out=BB[:, 16 + m, 0:nt], in_=ps[b][:, 0:nt], func=AF.Silu),
                         reads=[("ps", b)], writes=[("BB", 16 + m)], af="S")
                    yield

            def ret_gen():
                def stage_A(tc):
                    tsl = slice(tc * 128, (tc + 1) * 128)
                    sti = tc % 2
                    for half in range(2):
                        b = bank()

                        def fn(e, half=half, b=b):
                            ins = None
                            for j in range(4):
                                hc = half * 4 + j
                                ins = e.matmul(ps[b][:, j * 128:(j + 1) * 128], BB[:, 8 + hc, tsl], BB[:, hc, tsl], start=True, stop=True)
                            return ins
                        S.op("pe", fn, reads=[("BB", 8 + half * 4 + j) for j in range(4)] + [("BB", half * 4 + j) for j in range(4)],
                             writes=[("ps", b)])
                        S.op("dve", lambda e, half=half, b=b: e.tensor_tensor(
                            out=stT[sti][:, half * 4:(half + 1) * 4, :], in0=ps[b][:, :].rearrange("p (a c) -> p a c", a=4),
                            in1=maskT[:, half * 4:(half + 1) * 4, :], op=ALU.mult),
                            reads=[("ps", b), "maskT"], writes=[("stT", sti, half)])

                def stage_U(tc):
                    nb = (tc + 1) % 2
                    for half in range(2):
                        bk = bank()

                        def fn(e, half=half, bk=bk):
                            ins = None
                            for j in range(4):
                                hc = half * 4 + j
                                ins = e.matmul(ps[bk][:, j * 128:(j + 1) * 128], k_dec[:, tc, hc, :], v_tm[:, tc, hc * 128:(hc + 1) * 128],
                                               start=True, stop=True)
                            return ins
                        S.op("pe", fn, reads=[("k_dec", tc, half)] + [("v_tm", half * 4 + j) for j in range(4)], writes=[("ps", bk)])

                        def fn2(e, half=half, bk=bk):
                            ins = None
                            for j in range(4):
                                hc = half * 4 + j
                                ins = e.scalar_tensor_tensor(out=state[:, hc, :], in0=state[:, hc, :], scalar=float(CD[hc]),
                                                             in1=ps[bk][:, j * 128:(j + 1) * 128], op0=ALU.mult, op1=ALU.add)
                            return ins
                        S.op("dve", fn2, reads=[("ps", bk), ("state", half)], writes=[("state", half)])
                        S.op("act", lambda e, half=half: e.activation(out=state_bf[nb][:, half * 4:(half + 1) * 4, :],
                                                                      in_=state[:, half * 4:(half + 1) * 4, :], func=AF.Copy),
                             reads=[("state", half)], writes=[("state_bf", nb, half)])

                def stage_B(tc, half):
                    tsl = slice(tc * 128, (tc + 1) * 128)
                    sti = tc % 2
                    bo = bank()

                    def fn(e):
                        ins = None
                        for j in range(4):
                            hc = half * 4 + j
                            e.matmul(ps[bo][:, j * 128:(j + 1) * 128], v_tm[:, tc, hc * 128:(hc + 1) * 128], stT[sti][:, hc, :],
                                     start=True, stop=False)
                            ins = e.matmul(ps[bo][:, j * 128:(j + 1) * 128], state_bf[sti][:, hc, :], B2[:, hc, tsl],
                                           start=False, stop=True)
                        return ins
                    S.op("pe", fn, reads=[("v_tm", half * 4 + j) for j in range(4)] + [("stT", sti, half), ("state_bf", sti, half)]
                         + [("B2", half * 4 + j) for j in range(4)], writes=[("ps", bo)])
                    sq = bslot()
                    S.op("act", lambda e: e.activation(out=tmb[sq][:, :], in_=ps[bo][:, :], func=AF.Square),
                         reads=[("ps", bo)], writes=[("tmb", sq)])
                    return bo, sq

                def stage_C(tc, half, ctx):
                    tsl = slice(tc * 128, (tc + 1) * 128)
                    bo, sq = ctx
                    bs = bank()
                    mm(ps[bs][:, :], [(ones_bf[:, :], tmb[sq][:, :])], reads=["ones_bf", ("tmb", sq)], writes=[("ps", bs)])
                    t1 = tslot()
                    S.op("act", lambda e: e.activation(out=tmp[t1][:, :], in_=ps[bs][:, :], func=AF.Ln, scale=1.0 / 128, bias=EPS),
                         reads=[("ps", bs)], writes=[("tmp", t1)], af="L")
                    S.op("act", lambda e: e.activation(out=tmp[t1][:, :], in_=tmp[t1][:, :], func=AF.Exp, scale=-0.5),
                         reads=[("tmp", t1)], writes=[("tmp", t1)], af="E")
                    S.op("pool", lambda e: e.tensor_tensor(
                        out=tmp[t1][:, :].rearrange("p (a c) -> p a c", a=4), in0=tmp[t1][:, :].rearrange("p (a c) -> p a c", a=4),
                        in1=BB[:, 16 + half * 4:16 + (half + 1) * 4, tsl], op=ALU.mult),
                        reads=[("tmp", t1)] + [("BB", 16 + half * 4 + j) for j in range(4)], writes=[("tmp", t1)])
                    S.op("dve", lambda e: e.tensor_tensor(
                        out=yg[:, half * 4:(half + 1) * 4, tsl], in0=ps[bo][:, :].rearrange("p (a c) -> p a c", a=4),
                        in1=tmp[t1][:, :].rearrange("p (a c) -> p a c", a=4), op=ALU.mult),
                        reads=[("ps", bo), ("tmp", t1)], writes=[("yg", half * 4 + j) for j in range(4)])

                stage_A(0)
                yield
                for tc in range(nch):
                    stage_U(tc)
                    yield
                    if tc + 1 < nch:
                        stage_A(tc + 1)
                        yield
                    c0 = stage_B(tc, 0)
                    yield
                    c1 = stage_B(tc, 1)
                    yield
                    stage_C(tc, 0, c0)
                    yield
                    stage_C(tc, 1, c1)
                    yield

            def lru_gen():
                for g in range(4):
                    for j in range(2):
                        m = 2 * g + j
                        b = proj_fm("in", 32 + m, u_rhs, u_all, 8, nt)
                        S.op("pool", lambda e, j=j, m=m: e.tensor_copy(out=lin[:, j, 0:3], in_=halo[:, m, :]),
                             reads=["halo"], writes=[("lin", j)], c=0.2)
                        S.op("act", lambda e, j=j, b=b: e.activation(out=lin[:, j, 3:3 + nt], in_=ps[b][:, 0:nt], func=AF.Copy),
                             reads=[("ps", b)], writes=[("lin", j)])
                        S.op("act", lambda e, j=j, m=m, b=b: e.activation(out=cc[:, j, 0:nt], in_=ps[b][:, 0:nt], func=AF.Identity,
                                                                          scale=pv[:, 24 + 3 * 8 + m:24 + 3 * 8 + m + 1],
                                                                          bias=pv[:, 56 + m:56 + m + 1]),
                             reads=[("ps", b), "pv"], writes=[("cc", j)])
                        for tap in range(3):
                            S.op("dve", lambda e, j=j, m=m, tap=tap: e.scalar_tensor_tensor(
                                out=cc[:, j, 0:nt], in0=lin[:, j, tap:tap + nt], scalar=pv[:, 24 + tap * 8 + m:24 + tap * 8 + m + 1],
                                in1=cc[:, j, 0:nt], op0=ALU.mult, op1=ALU.add),
                                reads=[("lin", j), ("cc", j), "pv"], writes=[("cc", j)])
                        S.op("pool", lambda e, j=j, m=m: e.tensor_copy(out=halo[:, m, :], in_=lin[:, j, nt:nt + 3]),
                             reads=[("lin", j)], writes=["halo"], c=0.2)
                        S.op("act", lambda e, j=j: e.activation(out=cbf[:, j, 0:nt], in_=cc[:, j, 0:nt], func=AF.Copy),
                             reads=[("cc", j)], writes=[("cbf", j)])
                        yield
                    c_rhs = lambda kc: cbf[:, kc, 0:nt]
                    c_res = [("cbf", 0), ("cbf", 1)]
                    for j in range(2):
                        m = 2 * g + j
                        br_ = proj_fm("a", m, c_rhs, c_res, 2, nt)
                        thr = tslot()
                        S.op("act", lambda e, thr=thr, br_=br_, m=m: e.activation(out=tmp[thr][:, 0:nt], in_=ps[br_][:, 0:nt], func=AF.Tanh,
                                                                                  scale=0.5, bias=drv[:, m:m + 1]),
                             reads=[("ps", br_), "drv"], writes=[("tmp", thr)], af="T")
                        bi_ = proj_fm("x", m, c_rhs, c_res, 2, nt)
                        thi = tslot()
                        S.op("act", lambda e, thi=thi, bi_=bi_, m=m: e.activation(out=tmp[thi][:, 0:nt], in_=ps[bi_][:, 0:nt], func=AF.Tanh,
                                                                                  scale=0.5, bias=drv[:, 8 + m:8 + m + 1]),
                             reads=[("ps", bi_), "drv"], writes=[("tmp", thi)], af="T")
                        S.op("act", lambda e, thr=thr, j=j, m=m: e.activation(out=la[:, j, 0:nt], in_=tmp[thr][:, 0:nt], func=AF.Exp,
                                                                              scale=drv[:, 24 + m:24 + m + 1], bias=drv[:, 24 + m:24 + m + 1]),
                             reads=[("tmp", thr), "drv"], writes=[("la", j)], af="E")
                        S.op("act", lambda e, thr=thr, j=j, m=m: e.activation(out=la2[:, j, 0:nt], in_=tmp[thr][:, 0:nt], func=AF.Exp,
                                                                              scale=drv[:, 16 + m:16 + m + 1], bias=drv[:, 16 + m:16 + m + 1]),
                             reads=[("tmp", thr), "drv"], writes=[("la2", j)], af="E")
                        S.op("dve", lambda e, thi=thi, j=j: e.scalar_tensor_tensor(
                            out=luu[:, j, 0:nt], in0=tmp[thi][:, 0:nt], scalar=1.0, in1=cc[:, j, 0:nt], op0=ALU.add, op1=ALU.mult),
                            reads=[("tmp", thi), ("cc", j)], writes=[("luu", j)])
                        yield
                    for j in range(2):
                        m = 2 * g + j
                        S.op("act", lambda e, j=j: e.activation(out=la2[:, j, 0:nt], in_=la2[:, j, 0:nt], func=AF.Sqrt, scale=-0.25, bias=0.25),
                             reads=[("la2", j)], writes=[("la2", j)], af="Q")
                        S.op("dve", lambda e, j=j: e.tensor_tensor(out=luu[:, j, 0:nt], in0=luu[:, j, 0:nt], in1=la2[:, j, 0:nt], op=ALU.mult),
                             reads=[("luu", j), ("la2", j)], writes=[("luu", j)])
                        S.op("dve", lambda e, j=j, m=m: e.tensor_tensor_scan(out=la2[:, j, 0:nt], data0=la[:, j, 0:nt], data1=luu[:, j, 0:nt],
                                                                             initial=hst[:, m:m + 1], op0=ALU.mult, op1=ALU.add),
                             reads=[("la", j), ("luu", j), "hst", ("la2", j)], writes=[("la2", j)], c=1.15)
                        S.op("pool", lambda e, j=j, m=m: e.tensor_copy(out=hst[:, m:m + 1], in_=la2[:, j, nt - 1:nt]),
                             reads=[("la2", j)], writes=["hst"], c=0.2)
                        yield
                        if not is_meta:
                            b = proj_fm("in", 40 + m, u_rhs, u_all, 8, nt)
                            gl = tslot()
                            S.op("act", lambda e, gl=gl, b=b: e.activation(out=tmp[gl][:, 0:nt], in_=ps[b][:, 0:nt], func=AF.Gelu_apprx_tanh),
                                 reads=[("ps", b)], writes=[("tmp", gl)], af="G")
                            S.op("dve", lambda e, gl=gl, j=j, m=m: e.tensor_tensor(out=ylg[:, m, 0:nt], in0=tmp[gl][:, 0:nt], in1=la2[:, j, 0:nt],
                                                                                   op=ALU.mult),
                                 reads=[("tmp", gl), ("la2", j)], writes=[("ylg", m)])
                            yield

            yield from roundrobin([lru_gen(), None if is_meta else ret_gen()])
            if is_meta:
                return

            for m in range(8):
                bA = proj_fm("br", m, lambda kc: yg[:, kc, 0:nt], yg_all, 8, nt)
                bC = proj_fm("in", 48 + m, u_rhs, u_all, 8, nt)
                tha = tslot()
                S.op("act", lambda e, tha=tha, bC=bC: e.activation(out=tmp[tha][:, 0:nt], in_=ps[bC][:, 0:nt], func=AF.Tanh, scale=0.5),
                     reads=[("ps", bC)], writes=[("tmp", tha)], af="T")
                S.op("dve", lambda e, tha=tha, bA=bA: e.scalar_tensor_tensor(out=tmp[tha][:, 0:nt], in0=tmp[tha][:, 0:nt], scalar=1.0,
                                                                             in1=ps[bA][:, 0:nt], op0=ALU.add, op1=ALU.mult),
                     reads=[("tmp", tha), ("ps", bA)], writes=[("tmp", tha)])
                bB = proj_fm("bl", m, lambda kc: ylg[:, kc, 0:nt], ylg_all, 8, nt)
                bD = proj_fm("in", 56 + m, u_rhs, u_all, 8, nt)
                thb = tslot()
                S.op("act", lambda e, thb=thb, bD=bD: e.activation(out=tmp[thb][:, 0:nt], in_=ps[bD][:, 0:nt], func=AF.Tanh, scale=0.5),
                     reads=[("ps", bD)], writes=[("tmp", thb)], af="T")
                S.op("dve", lambda e, thb=thb, bB=bB: e.scalar_tensor_tensor(out=tmp[thb][:, 0:nt], in0=tmp[thb][:, 0:nt], scalar=1.0,
                                                                             in1=ps[bB][:, 0:nt], op0=ALU.add, op1=ALU.mult),
                     reads=[("tmp", thb), ("ps", bB)], writes=[("tmp", thb)])
                S.op("pool", lambda e, tha=tha, thb=thb, m=m: e.tensor_tensor(out=BB[:, 8 + m, 0:nt], in0=tmp[tha][:, 0:nt],
                                                                              in1=tmp[thb][:, 0:nt], op=ALU.add),
                     reads=[("tmp", tha), ("tmp", thb)], writes=[("BB", 8 + m)])
                yield
            mx_all = [("BB", 8 + k) for k in range(8)]
            for m in range(8):
                b = proj_fm("o", m, lambda kc: BB[:, 8 + kc, 0:nt], mx_all, 8, nt)
                S.op("dve", lambda e, m=m, b=b: e.scalar_tensor_tensor(out=hb[:, m, 0:nt], in0=ps[b][:, 0:nt], scalar=0.5, in1=hb[:, m, 0:nt],
                                                                       op0=ALU.mult, op1=ALU.add),
                     reads=[("ps", b), (hk, m)], writes=[(hk, m)])
                yield
            yield from rmsnorm(hb, hk, nt, 8, u, "u", lambda fc: u[:, fc, 0:nt], u_res)
            for jf in range(NFF):
                bG = proj_fm("fi", jf, u_rhs, u_all, 8, nt)
                sg_ = tslot()
                S.op("act", lambda e, sg_=sg_, bG=bG: e.activation(out=tmp[sg_][:, 0:nt], in_=ps[bG][:, 0:nt], func=AF.Silu),
                     reads=[("ps", bG)], writes=[("tmp", sg_)], af="S")
                bU = proj_fm("fi", NFF + jf, u_rhs, u_all, 8, nt)
                S.op("dve", lambda e, sg_=sg_, bU=bU, jf=jf: e.tensor_tensor(out=BB[:, jf, 0:nt], in0=tmp[sg_][:, 0:nt], in1=ps[bU][:, 0:nt],
                                                                             op=ALU.mult),
                     reads=[("tmp", sg_), ("ps", bU)], writes=[("BB", jf)])
                yield

        def tile_tail(ti):
            nt = NT
            nch = NT // 128
            hb = hh[ti % 2]
            hk = "h%d" % (ti % 2)
            act_all = [("BB", k) for k in range(NFF)]
            for m in range(8):
                b = proj_fm("fo", m, lambda kc: BB[:, kc, 0:nt], act_all, NFF, nt)
                S.op("dve", lambda e, m=m, b=b: e.tensor_tensor(out=hb[:, m, 0:nt], in0=ps[b][:, 0:nt], in1=hb[:, m, 0:nt], op=ALU.add),
                     reads=[("ps", b), (hk, m)], writes=[(hk, m)])
                yield
            yield from rmsnorm(hb, hk, nt, 16, yg, "yg", lambda fc: hb[:, fc, 0:nt], lambda fc: (hk, fc))
            for tc in range(nch):
                ob = tc % 2
                for half in range(2):
                    b = bank()

                    def fn(e, half=half, b=b, tc=tc):
                        ins = None
                        for j in range(4):
                            fc = half * 4 + j
                            ins = e.transpose(out=ps[b][:, j * 128:(j + 1) * 128], in_=hb[:, fc, tc * 128:(tc + 1) * 128], identity=ident[:, :])
                        return ins
                    S.op("pe", fn, reads=[(hk, half * 4 + j) for j in range(4)] + ["ident"], writes=[("ps", b)])
                    if half == 0:
                        S.op("act", lambda e, b=b, ob=ob: e.activation(out=otm[ob][:, 0:512], in_=ps[b][:, :], func=AF.Copy),
                             reads=[("ps", b)], writes=[("otm", ob, 0)])
                    else:
                        S.op("dve", lambda e, b=b, ob=ob: e.tensor_copy(out=otm[ob][:, 512:1024], in_=ps[b][:, :]),
                             reads=[("ps", b)], writes=[("otm", ob, 1)])
                r0 = ti * NT + tc * 128
                S.dma("sp", lambda e, r0=r0, ob=ob: e.dma_start(out=out[r0:r0 + 128, :], in_=otm[ob][:, :]), ("ost", ob),
                      reads=[("otm", ob, 0), ("otm", ob, 1)], writes=[("out", ob)])
                yield

        log_g = np.log(1.0 - 2.0 ** (-5.0 - np.arange(8, dtype=np.float64)))
        CD = np.exp(128.0 * log_g)

        def run_all(g):
            for _ in g:
                pass

        run_all(tile_head(-1))
        run_all(tile_head(0))
        for ti in range(n_tiles):
            run_all(roundrobin([tile_tail(ti), tile_head(ti + 1) if ti + 1 < n_tiles else None]))
        with nc.Block() as block:
            S.emit(block)
    return nc


def host_consts(T):
    inv = (np.float32(10000.0) ** (-(np.arange(0, 128, 2, dtype=np.float32)) / np.float32(128))).astype(np.float32)
    pos = np.arange(T, dtype=np.float32)
    ang = (pos[None, :] * inv[:, None]).astype(np.float32).astype(np.float64)
    cos, sin = np.cos(ang), np.sin(ang)
    cs = np.stack([np.concatenate([cos, cos], 0), np.concatenate([-sin, sin], 0)], 1).astype(np.float32)
    log_g = np.log(1.0 - 2.0 ** (-5.0 - np.arange(8, dtype=np.float64)))
    idx = np.arange(128)
    diff = idx[None, :] - idx[:, None]
    maskT = np.where(diff[None] >= 0, np.exp(np.maximum(diff, 0)[None] * log_g[:, None, None]), 0.0) * 128.0 ** -0.5
    maskT = np.ascontiguousarray(maskT.transpose(1, 0, 2)).astype(np.float32)
    qd = np.exp((idx + 1.0)[None, :] * log_g[:, None])
    qd = np.ascontiguousarray(np.broadcast_to(qd[None], (128, 8, 128))).astype(np.float32)
    kd = np.zeros((128, 16), np.float64)
    kd[:, 0:8] = np.exp((127.0 - idx)[:, None] * log_g[None, :]) * 128.0 ** -0.5
    kd[:16, 8:16] = np.exp((15.0 - np.arange(16))[:, None] * log_g[None, :]) * 128.0 ** -0.5
    ident = np.eye(128, dtype=np.float32)
    swapP = np.zeros((128, 128), np.float32)
    swapP[(np.arange(128) + 64) % 128, np.arange(128)] = 1.0
    return dict(cs_tab=cs, maskT=maskT, qd=qd, kd=kd.astype(np.float32), ident=ident, swapP=swapP)


def col_layout(v):
    return np.ascontiguousarray(np.asarray(v, np.float32).reshape(8, 128).T)


_NC_CACHE = {}


def run(inputs, n_tiles, n_cores):
    x = np.asarray(inputs["x"], np.float32)
    T = NMETA + n_tiles * NT
    consts = host_consts(T)
    cw = np.asarray(inputs["conv_w"], np.float32)[0]
    pv = np.concatenate(
        [col_layout(inputs["mix_norm_w"][0]), col_layout(inputs["ffn_norm_w"][0]), col_layout(inputs["final_norm_w"])]
        + [col_layout(cw[j]) for j in range(4)]
        + [col_layout(inputs["conv_b"][0]), col_layout(inputs["lru_ba"][0]), col_layout(inputs["lru_bx"][0]),
           col_layout(inputs["lru_lambda"][0])], axis=1)
    assert pv.shape == (128, 88)
    shared = dict(
        meta_tokens=np.asarray(inputs["meta_tokens"], np.float32),
        w_in=np.asarray(inputs["w_in"], np.float32)[0],
        w_branch_ret=np.asarray(inputs["w_branch_ret"], np.float32)[0],
        w_branch_lru=np.asarray(inputs["w_branch_lru"], np.float32)[0],
        w_out=np.asarray(inputs["w_out"], np.float32)[0],
        lru_wa=np.asarray(inputs["lru_wa"], np.float32)[0],
        lru_wx=np.asarray(inputs["lru_wx"], np.float32)[0],
        w_ffn_in=np.asarray(inputs["w_ffn_in"], np.float32)[0],
        w_ffn_out=np.asarray(inputs["w_ffn_out"], np.float32)[0],
        pv=np.ascontiguousarray(pv), **consts)
    if n_tiles not in _NC_CACHE:
        _NC_CACHE[n_tiles] = build(n_tiles)
    nc = _NC_CACHE[n_tiles]
    in_maps = [dict(shared, x=np.ascontiguousarray(x[b, :n_tiles * NT])) for b in range(n_cores)]
    res = run_bass_kernel_spmd(nc, in_maps, core_ids=list(range(n_cores)))
    return np.stack([np.asarray(res.results[b]["out"]) for b in range(n_cores)], 0).astype(np.float32)


def kernel(**inputs):
    x = inputs["x"]
    return run(inputs, x.shape[1] // NT, x.shape[0])
```
